# Optimizing a Trainium2 kernel written in Bass

```python
import jax, jax.numpy as jnp
from jax import lax
import numpy as np

D_MODEL = 2048
BATCH = 1
SEQ = 8192
DEPTH = 1

GRID_W = 64
MIX_WIDTH = D_MODEL
MLA_HEADS = 8
MLA_NOPE = 128
MLA_ROPE = 64
MLA_V = 128
MLA_Q_RANK = 512
MLA_KV_RANK = 256
ROPE_THETA = 10000.0
NA_HEADS = 8
NA_HEAD_DIM = 128
NA_ROWS_MAX = 8
NA_COLS = 16
D_FF = 4 * D_MODEL
Q_BLOCK = 128
NORM_EPS = 1e-6

MLA_WIDTH = MLA_HEADS * MLA_V
NA_WIDTH = NA_HEADS * NA_HEAD_DIM
MLA_QK = MLA_NOPE + MLA_ROPE
IN_COLS = MLA_Q_RANK + MLA_KV_RANK + MLA_ROPE + 3 * NA_WIDTH

kernel_name = "hybrid_mla_natten_sqrelu_block"


def rms_norm(x, g):
    xf = x.astype(jnp.float32)
    y = xf * lax.rsqrt(jnp.mean(xf * xf, axis=-1, keepdims=True) + NORM_EPS)
    return (y * g.astype(jnp.float32)).astype(x.dtype)


def apply_rope(x, pos):
    half = x.shape[-1] // 2
    inv = ROPE_THETA ** (-jnp.arange(half, dtype=jnp.float32) / half)
    ang = pos.astype(jnp.float32)[:, None] * inv[None, :]
    c = jnp.cos(ang)[None, :, None, :]
    s = jnp.sin(ang)[None, :, None, :]
    xf = x.astype(jnp.float32)
    x1, x2 = xf[..., :half], xf[..., half:]
    return jnp.concatenate([x1 * c - x2 * s, x1 * s + x2 * c], axis=-1).astype(x.dtype)


def mla_mixer(c_q, c_kv, k_rope, q_norm_g, w_uq, kv_norm_g, w_ukv):
    B, S, _ = c_q.shape
    H = MLA_HEADS
    pos = jnp.arange(S)
    q = (rms_norm(c_q, q_norm_g) @ w_uq).reshape(B, S, H, MLA_QK)
    q_nope, q_rope = q[..., :MLA_NOPE], apply_rope(q[..., MLA_NOPE:], pos)
    kv = (rms_norm(c_kv, kv_norm_g) @ w_ukv).reshape(B, S, H, MLA_NOPE + MLA_V)
    k_nope, v = kv[..., :MLA_NOPE], kv[..., MLA_NOPE:]
    k_r = apply_rope(k_rope[:, :, None, :], pos)[:, :, 0, :]
    scale = MLA_QK ** -0.5
    nb = S // Q_BLOCK
    qn_b = q_nope.reshape(B, nb, Q_BLOCK, H, MLA_NOPE).transpose(1, 0, 2, 3, 4)
    qr_b = q_rope.reshape(B, nb, Q_BLOCK, H, MLA_ROPE).transpose(1, 0, 2, 3, 4)

    def block(args):
        qn, qr = args
        s = (jnp.einsum('bqhd,bkhd->bhqk', qn, k_nope, preferred_element_type=jnp.float32)
             + jnp.einsum('bqhr,bkr->bhqk', qr, k_r, preferred_element_type=jnp.float32))
        p = jax.nn.softmax(s * scale, axis=-1).astype(v.dtype)
        return jnp.einsum('bhqk,bkhd->bqhd', p, v)

    o = lax.map(block, (qn_b, qr_b))
    return o.transpose(1, 0, 2, 3, 4).reshape(B, S, MLA_WIDTH)


def na_mixer(q, k, v, rpb):
    B, S, _ = q.shape
    W = GRID_W
    rows = S // W
    kr = min(NA_ROWS_MAX, rows)
    kc = NA_COLS
    H, D = NA_HEADS, NA_HEAD_DIM
    qg = q.reshape(B, rows, W, H, D).transpose(1, 0, 2, 3, 4)
    kg = k.reshape(B, rows, W, H, D)
    vg = v.reshape(B, rows, W, H, D)
    col = jnp.arange(W)
    col_start = jnp.clip(col - kc // 2, 0, W - kc)
    col_idx = col_start[:, None] + jnp.arange(kc)[None, :]
    dc = col_idx - col[:, None] + (NA_COLS - 1)
    scale = D ** -0.5

    def row_block(args):
        r, qr = args
        rs = jnp.clip(r - kr // 2, 0, rows - kr)
        kb = lax.dynamic_slice_in_dim(kg, rs, kr, axis=1)
        vb = lax.dynamic_slice_in_dim(vg, rs, kr, axis=1)
        kw = kb[:, :, col_idx]
        vw = vb[:, :, col_idx]
        dr = rs + jnp.arange(kr) - r + (NA_ROWS_MAX - 1)
        bias = rpb[:, dr[:, None, None], dc[None, :, :]]
        bias = bias.transpose(0, 2, 1, 3).astype(jnp.float32)
        s = jnp.einsum('bchd,bicjhd->bhcij', qr, kw, preferred_element_type=jnp.float32) * scale
        s = (s + bias[None]).reshape(B, H, W, kr * kc)
        p = jax.nn.softmax(s, axis=-1).astype(vw.dtype).reshape(B, H, W, kr, kc)
        return jnp.einsum('bhcij,bicjhd->bchd', p, vw)

    o = lax.map(row_block, (jnp.arange(rows), qg))
    return o.transpose(1, 0, 2, 3, 4).reshape(B, S, NA_WIDTH)


def setup_inputs(seed: int = 0) -> dict:
    key = jax.random.key(seed)
    ks = jax.random.split(key, 20)
    f32 = jnp.float32

    def nrm(k, shape, scale):
        return jax.random.normal(k, shape, f32) * scale

    def gain(k, n):
        return 1.0 + 0.02 * jax.random.normal(k, (DEPTH, n), f32)

    return {
        "x": jax.random.normal(ks[0], (BATCH, SEQ, D_MODEL), f32),
        "attn_norm_g": gain(ks[1], D_MODEL),
        "w_in": nrm(ks[2], (DEPTH, D_MODEL, IN_COLS), D_MODEL ** -0.5),
        "q_norm_g": gain(ks[3], MLA_Q_RANK),
        "w_uq": nrm(ks[4], (DEPTH, MLA_Q_RANK, MLA_HEADS * MLA_QK), MLA_Q_RANK ** -0.5),
        "kv_norm_g": gain(ks[5], MLA_KV_RANK),
        "w_ukv": nrm(ks[6], (DEPTH, MLA_KV_RANK, MLA_HEADS * (MLA_NOPE + MLA_V)), MLA_KV_RANK ** -0.5),
        "na_rpb": nrm(ks[7], (DEPTH, NA_HEADS, 2 * NA_ROWS_MAX - 1, 2 * NA_COLS - 1), 0.5),
        "mla_out_norm_g": gain(ks[8], MLA_WIDTH),
        "na_out_norm_g": gain(ks[9], NA_WIDTH),
        "w_out": nrm(ks[10], (DEPTH, MIX_WIDTH, D_MODEL), MIX_WIDTH ** -0.5),
        "mlp_norm_g": gain(ks[11], D_MODEL),
        "w_ff1": nrm(ks[12], (DEPTH, D_MODEL, D_FF), D_MODEL ** -0.5),
        "w_ff2": nrm(ks[13], (DEPTH, D_FF, D_MODEL), (1.5 * D_FF) ** -0.5),
        "final_norm_g": 1.0 + 0.02 * jax.random.normal(ks[14], (D_MODEL,), f32),
    }


def reference(x, attn_norm_g, w_in, q_norm_g, w_uq, kv_norm_g, w_ukv, na_rpb,
              mla_out_norm_g, na_out_norm_g, w_out, mlp_norm_g, w_ff1, w_ff2,
              final_norm_g):
    splits = np.cumsum([MLA_Q_RANK, MLA_KV_RANK, MLA_ROPE, NA_WIDTH, NA_WIDTH]).tolist()
    for l in range(DEPTH):
        h = rms_norm(x, attn_norm_g[l])
        z = h @ w_in[l]
        c_q, c_kv, k_rope, na_q, na_k, na_v = jnp.split(z, splits, axis=-1)
        a_out = mla_mixer(c_q, c_kv, k_rope, q_norm_g[l], w_uq[l], kv_norm_g[l], w_ukv[l])
        n_out = na_mixer(na_q, na_k, na_v, na_rpb[l])
        mixed = jnp.concatenate([rms_norm(a_out, mla_out_norm_g[l]),
                                 rms_norm(n_out, na_out_norm_g[l])], axis=-1)
        x = x + mixed @ w_out[l]
        h = rms_norm(x, mlp_norm_g[l])
        x = x + jnp.square(jax.nn.relu(h @ w_ff1[l])) @ w_ff2[l]
    return rms_norm(x, final_norm_g)
```

```python
import contextlib
import numpy as np
import concourse.bass as bass
import concourse.mybir as mybir
from concourse.bass_utils import run_bass_kernel_spmd

F32 = mybir.dt.float32
BF16 = mybir.dt.bfloat16
ALU = mybir.AluOpType
AF = mybir.ActivationFunctionType

NCORES = 8
S = 8192
D = 2048
TOK = S // NCORES
EPS = 1e-6
KIB = 1024
ARENA = 204 * KIB
CH = 512
NEG = -30000.0


class Prog:
    ENGS = ("pe", "act", "dve", "pool", "sp")
    NDMA = {"sp": 8, "pool": 8, "act": 4}

    def __init__(self, nc):
        self.nc = nc
        self.ops = []
        self.enabled = True

    def op(self, eng, fn, reads=(), writes=(), dma=False, force=False):
        if not (self.enabled or force):
            return
        self.ops.append(dict(eng=eng, fn=fn, reads=tuple(reads), writes=tuple(writes), dma=dma))

    def dma(self, q, out, in_, reads=(), writes=(), force=False):
        self.op(q, lambda e: e.dma_start(out=out, in_=in_), reads, writes, dma=True, force=force)

    def fence(self, eng, reads):
        self.op(eng, None, reads, (), force=True)

    def finish(self, stack):
        nc = self.nc
        ops = self.ops
        n = len(ops)
        lw = {}
        rd = {}
        need = [None] * n
        milestone = [False] * n
        lastacc = {}
        for i, o in enumerate(ops):
            d = set()
            excl = set(k for k in o["reads"] + o["writes"] if isinstance(k, tuple) and k[0] == "ps")
            for k in excl:
                la = lastacc.setdefault(k, {})
                for e2, j in la.items():
                    if e2 != o["eng"]:
                        d.add(j)
                la[o["eng"]] = i
            for r in o["reads"]:
                if r in excl:
                    continue
                j = lw.get(r)
                if j is not None:
                    d.add(j)
            for w in o["writes"]:
                if w in excl:
                    continue
                j = lw.get(w)
                if j is not None:
                    d.add(j)
                for j in rd.get(w, ()):
                    if ops[j]["eng"] == o["eng"] and o["eng"] in ("act", "dve") and not ops[j]["dma"] and not o["dma"]:
                        continue
                    d.add(j)
            d.discard(i)
            best = {}
            for j in d:
                oj = ops[j]
                if oj["dma"]:
                    best[("dma", j)] = j
                else:
                    if oj["eng"] == "pe" and o["eng"] == "pe" and not o["dma"]:
                        continue
                    key = ("eng", oj["eng"])
                    if key not in best or best[key] < j:
                        best[key] = j
            need[i] = sorted(best.values())
            for j in need[i]:
                if not ops[j]["dma"]:
                    milestone[j] = True
            for r in o["reads"]:
                if r not in excl:
                    rd.setdefault(r, []).append(i)
            for w in o["writes"]:
                if w not in excl:
                    lw[w] = i
                    rd[w] = []
        sems = {}
        for e in ("pe", "act", "dve", "pool"):
            sems[e] = stack.enter_context(nc.semaphore("s_" + e))
        dsems = {}
        for q, k in self.NDMA.items():
            dsems[q] = [stack.enter_context(nc.semaphore("d_%s%d" % (q, t))) for t in range(k)]
        tok = [None] * n
        cnt = {e: 0 for e in sems}
        dcount = {q: 0 for q in dsems}
        dval = {q: [0] * len(dsems[q]) for q in dsems}
        for i, o in enumerate(ops):
            if o["fn"] is None:
                continue
            if o["dma"]:
                q = o["eng"]
                t = dcount[q] % len(dsems[q])
                dcount[q] += 1
                dval[q][t] += 16
                tok[i] = (dsems[q][t], dval[q][t])
            elif milestone[i]:
                cnt[o["eng"]] += 1
                tok[i] = (sems[o["eng"]], cnt[o["eng"]])
        self.stats = dict(n_ops=n, milestones=dict(cnt), dmas=dict(dcount))
        per_eng = {e: [] for e in self.ENGS}
        for i, o in enumerate(ops):
            per_eng[o["eng"]].append(i)

        def emit(engname, e):
            waited = {}
            for i in per_eng[engname]:
                o = ops[i]
                for j in need[i]:
                    s, v = tok[j]
                    key = id(s)
                    if waited.get(key, 0) >= v:
                        continue
                    waited[key] = v
                    e.wait_ge(s, v)
                if o["fn"] is None:
                    continue
                ins = o["fn"](e)
                if tok[i] is not None:
                    ins.then_inc(tok[i][0], 16 if o["dma"] else 1)

        block = stack.enter_context(nc.Block())

        @block.tensor
        def _(e):
            emit("pe", e)

        @block.scalar
        def _(e):
            emit("act", e)

        @block.vector
        def _(e):
            emit("dve", e)

        @block.gpsimd
        def _(e):
            emit("pool", e)

        @block.sync
        def _(e):
            emit("sp", e)


class View:
    def __init__(self, arena, off, dtype, shape):
        self.off = off
        self.es = 2 if dtype == BF16 else 4
        self.shape = tuple(shape)
        n = 1
        for s in shape:
            n *= s
        self.n = n
        self.nbytes = n * self.es
        assert off % 4 == 0 and off + self.nbytes <= ARENA, (off, self.nbytes)
        a = arena[:, off // 2:(off + self.nbytes) // 2]
        if dtype != BF16:
            a = a.bitcast(dtype)
        if len(shape) == 2:
            a = a.rearrange("p (a b) -> p a b", a=shape[0])
        elif len(shape) == 3:
            a = a.rearrange("p (a b c) -> p a b c", a=shape[0], b=shape[1])
        elif len(shape) == 4:
            a = a.rearrange("p (a b c d) -> p a b c d", a=shape[0], b=shape[1], c=shape[2])
        self.ap = a

    def r(self, lo, hi):
        b0 = (self.off + lo * self.es) // CH
        b1 = (self.off + hi * self.es - 1) // CH
        return list(range(b0, b1 + 1))

    def k(self):
        return self.r(0, self.n)


class Bump:
    def __init__(self, arena, base_kib):
        self.arena = arena
        self.o = int(base_kib * KIB)

    def take(self, dtype, shape):
        v = View(self.arena, self.o, dtype, shape)
        self.o += (v.nbytes + CH - 1) // CH * CH
        return v


def build_program(debug=False, stop=None):
    nc = bass.Bass("TRN2", target_bir_lowering=False)
    DR = {}

    def din(name, shape):
        DR[name] = nc.dram_tensor(name, list(shape), F32, kind="ExternalInput").ap()

    din("xT", [128, 16, S])
    din("xown", [TOK, D])
    din("cs", [64, 2, S])
    din("wa", [128, 16, 384])
    din("wcq", [128, 16, 512])
    din("wna", [128, 16, 3072])
    din("wuq", [128, 4, 1536])
    din("wuqs", [128, 4, 512])
    din("wukv", [128, 2, 2048])
    din("wout", [128, 16, 2048])
    din("wff1", [128, 16, 8192])
    din("wff2", [128, 64, 2048])
    din("gvec", [128, 22])
    din("gbc", [128, 6144])
    din("nab", [8, 128, 3840])
    din("ident", [128, 128])
    y = nc.dram_tensor("y", [TOK, D], F32, kind="ExternalOutput").ap()
    dbg = {}
    if debug:
        for name, shape in (("d_ckv", [128, 2 * S]), ("d_krt", [64, S]), ("d_nout", [128, 8 * 1024]),
                            ("d_aout", [128, 8 * 1024]), ("d_x1", [128, 8 * 2048]), ("d_qn", [128, 8 * 1024]),
                            ("d_qr", [64, 8 * 1024])):
            dbg[name] = nc.dram_tensor(name, shape, F32 if name in ("d_nout", "d_aout", "d_x1") else BF16,
                                       kind="ExternalOutput").ap()

    st = contextlib.ExitStack()
    with st:
        arena = st.enter_context(nc.sbuf_tensor("arena", [128, ARENA // 2], BF16))
        psum = st.enter_context(nc.psum_tensor("ps", [128, 8, 512], F32))
        P = Prog(nc)

        def pk(b, lo=0, hi=512):
            return [("ps", b)]

        bp = Bump(arena, 0)
        ONES = bp.take(BF16, [128])
        IDENT = bp.take(BF16, [128])
        GV = bp.take(F32, [22])
        EPSV = bp.take(F32, [1])
        SS7 = [bp.take(F32, [1]) for _ in range(2)]
        assert bp.o <= 8 * KIB
        CKV = View(arena, 8 * KIB, BF16, [2, S])
        KRT = View(arena, 40 * KIB, BF16, [S])
        MIXT = View(arena, 56 * KIB, BF16, [16, TOK])
        QN = View(arena, 172 * KIB, BF16, [8, TOK])
        QR = View(arena, 188 * KIB, BF16, [8, TOK])

        P.op("dve", lambda e: e.memset(ONES.ap, 1.0), writes=ONES.k())
        P.op("dve", lambda e: e.memset(EPSV.ap, EPS), writes=EPSV.k())
        P.op("pool", lambda e: e.memset(KRT.ap[64:128], 0.0), writes=KRT.k())
        P.dma("sp", GV.ap, DR["gvec"], writes=GV.k())
        P.dma("pool", IDENT.ap, DR["ident"], writes=IDENT.k())

        evac_rr = [0]

        def evac(out_ap, in_ap, reads, writes, eng=None):
            if eng is None:
                eng = ("act", "dve")[evac_rr[0] % 2]
                evac_rr[0] += 1
            if eng == "act":
                P.op("act", lambda e: e.activation(out=out_ap, in_=in_ap, func=AF.Copy), reads, writes)
            else:
                P.op("dve", lambda e: e.tensor_copy(out=out_ap, in_=in_ap), reads, writes)

        def rstd_from_ss(dst, src_ap, src_keys, inv_n):
            P.op("act", lambda e: e.activation(out=dst.ap, in_=src_ap, func=AF.Ln, scale=inv_n, bias=EPSV.ap[0:dst.ap.shape[0]]),
                 reads=list(src_keys) + EPSV.k(), writes=dst.k())
            P.op("act", lambda e: e.activation(out=dst.ap, in_=dst.ap, func=AF.Exp, scale=-0.5), reads=dst.k(), writes=dst.k())

        def alloc_tile_bufs(base_kib, nxt, nsq, nht, kv=True):
            b = Bump(arena, base_kib)
            tb = dict()
            tb["XT"] = [b.take(F32, [16, 256]) for _ in range(nxt)]
            tb["SQ"] = [b.take(BF16, [16, 256]) for _ in range(nsq)]
            tb["HT"] = [b.take(BF16, [16, 256]) for _ in range(nht)]
            tb["RS"] = [b.take(F32, [256]) for _ in range(2)]
            if kv:
                tb["WA"] = b.take(BF16, [16, 384])
                tb["CS"] = [b.take(F32, [2, 256]) for _ in range(2)]
                tb["R2"] = [b.take(F32, [256]) for _ in range(2)]
                tb["SQ2"] = [b.take(BF16, [2, 256]) for _ in range(2)]
                tb["T1"] = [b.take(F32, [256]) for _ in range(1)]
                tb["T2"] = [b.take(F32, [256]) for _ in range(1)]
            tb["end"] = b.o
            return tb

        tile_ctr = [0]

        def proc_tile(tb, ti, ht, do_kv):
            c = tile_ctr[0]
            tile_ctr[0] += 1
            stage_a1(tb, ti, ht, c)
            stage_a2(tb, ti, ht, c)
            if do_kv:
                stage_b_mm(tb, ti, ht, c)
                stage_b_tail(tb, ti, ht, c)

        def run_tiles(tb, tiles, hts):
            cs_ = []
            for _ in tiles:
                cs_.append(tile_ctr[0])
                tile_ctr[0] += 1
            n_ = len(tiles)
            stage_a1(tb, tiles[0], hts[0], cs_[0])
            if n_ > 1:
                stage_a1(tb, tiles[1], hts[1], cs_[1])
            stage_a2(tb, tiles[0], hts[0], cs_[0])
            for i in range(n_):
                stage_b_mm(tb, tiles[i], hts[i], cs_[i])
                if i + 2 < n_:
                    stage_a1(tb, tiles[i + 2], hts[i + 2], cs_[i + 2])
                if i + 1 < n_:
                    stage_a2(tb, tiles[i + 1], hts[i + 1], cs_[i + 1])
                stage_b_tail(tb, tiles[i], hts[i], cs_[i])

        def stage_a1(tb, ti, ht, c):
            t0 = ti * 256
            XT = tb["XT"][c % len(tb["XT"])]
            SQ = tb["SQ"][c % len(tb["SQ"])]
            bA = (0, 1, 6)[c % 3]
            P.dma("sp", XT.ap, DR["xT"][:, :, t0:t0 + 256], writes=XT.k())
            P.op("act", lambda e: e.activation(out=SQ.ap, in_=XT.ap, func=AF.Square), reads=XT.k(), writes=SQ.k())
            for kc in range(16):
                P.op("pe", lambda e, kc=kc: e.matmul(psum[:, bA, 0:256], lhsT=ONES.ap, rhs=SQ.ap[:, kc, :],
                                                     start=(kc == 0), stop=(kc == 15)),
                     reads=ONES.k() + SQ.r(kc * 256, kc * 256 + 256), writes=pk(bA, 0, 256))

        def stage_a2(tb, ti, ht, c):
            XT = tb["XT"][c % len(tb["XT"])]
            RS = tb["RS"][c % 2]
            bA = (0, 1, 6)[c % 3]
            rstd_from_ss(RS, psum[:, bA, 0:256], pk(bA, 0, 256), 1.0 / D)
            for kc in range(16):
                P.op("dve", lambda e, kc=kc: e.scalar_tensor_tensor(out=ht.ap[:, kc, :], in0=XT.ap[:, kc, :],
                                                                 scalar=GV.ap[:, kc:kc + 1], in1=RS.ap,
                                                                 op0=ALU.mult, op1=ALU.mult),
                     reads=XT.r(kc * 256, kc * 256 + 256) + GV.k() + RS.k(), writes=ht.r(kc * 256, kc * 256 + 256))

        def stage_b_mm(tb, ti, ht, c):
            bB = 2 + (c % 2)
            bC = 4 + (c % 2)
            WA = tb["WA"]
            for m in range(2):
                for kc in range(16):
                    P.op("pe", lambda e, kc=kc, m=m: e.matmul(psum[:, bB, m * 256:(m + 1) * 256],
                                                              lhsT=WA.ap[:, kc, m * 128:(m + 1) * 128], rhs=ht.ap[:, kc, :],
                                                              start=(kc == 0), stop=(kc == 15)),
                         reads=WA.k() + ht.r(kc * 256, kc * 256 + 256), writes=pk(bB, m * 256, m * 256 + 256))
            for m in range(2):
                for kc in range(16):
                    P.op("pe", lambda e, kc=kc, m=m: e.matmul(psum[0:64, bC, m * 256:(m + 1) * 256],
                                                              lhsT=WA.ap[:, kc, 256 + m * 64:256 + (m + 1) * 64], rhs=ht.ap[:, kc, :],
                                                              start=(kc == 0), stop=(kc == 15)),
                         reads=WA.k() + ht.r(kc * 256, kc * 256 + 256), writes=pk(bC, m * 256, m * 256 + 256))

        def stage_b_tail(tb, ti, ht, c):
            t0 = ti * 256
            bA = (0, 1, 6)[c % 3]
            bB = 2 + (c % 2)
            bC = 4 + (c % 2)
            CSB = tb["CS"][c % 2]
            R2 = tb["R2"][c % 2]
            SQ2 = tb["SQ2"][c % 2]
            T1 = tb["T1"][0]
            T2 = tb["T2"][0]
            P.dma("sp", CSB.ap[0:64], DR["cs"][:, :, t0:t0 + 256], writes=CSB.k())
            P.op("act", lambda e: e.activation(out=SQ2.ap, in_=psum[:, bB, :].rearrange("p (m t) -> p m t", m=2), func=AF.Square),
                 reads=pk(bB), writes=SQ2.k())
            for m in range(2):
                P.op("pe", lambda e, m=m: e.matmul(psum[:, bA, 256:512], lhsT=ONES.ap, rhs=SQ2.ap[:, m, :],
                                                   start=(m == 0), stop=(m == 1)),
                     reads=ONES.k() + SQ2.k(), writes=pk(bA, 256, 512))
            rstd_from_ss(R2, psum[:, bA, 256:512], pk(bA, 256, 512), 1.0 / 256)
            for m in range(2):
                P.op("dve", lambda e, m=m: e.scalar_tensor_tensor(out=CKV.ap[:, m, t0:t0 + 256], in0=psum[:, bB, m * 256:(m + 1) * 256],
                                                                 scalar=GV.ap[:, 20 + m:21 + m], in1=R2.ap,
                                                                 op0=ALU.mult, op1=ALU.mult),
                     reads=pk(bB, m * 256, m * 256 + 256) + GV.k() + R2.k(), writes=CKV.r(m * S + t0, m * S + t0 + 256))
            P.op("dve", lambda e: e.tensor_tensor(out=T1.ap[0:64], in0=psum[0:64, bC, 0:256], in1=CSB.ap[0:64, 0, :], op=ALU.mult),
                 reads=pk(bC, 0, 256) + CSB.k(), writes=T1.k())
            P.op("dve", lambda e: e.tensor_tensor(out=T2.ap[0:64], in0=psum[0:64, bC, 256:512], in1=CSB.ap[0:64, 1, :], op=ALU.mult),
                 reads=pk(bC, 256, 512) + CSB.k(), writes=T2.k())
            P.op("pool", lambda e: e.tensor_tensor(out=KRT.ap[0:64, t0:t0 + 256], in0=T1.ap[0:64], in1=T2.ap[0:64], op=ALU.add),
                 reads=T1.k() + T2.k(), writes=KRT.r(t0, t0 + 256))

        tb1 = alloc_tile_bufs(56, 3, 1, 0)
        assert tb1["end"] <= 156 * KIB, tb1["end"]
        HTN = [View(arena, (156 + 8 * j) * KIB, BF16, [16, 256]) for j in range(6)]
        if stop == 0:
            P.enabled = False
        P.dma("pool", tb1["WA"].ap, DR["wa"], writes=tb1["WA"].k())
        na_tiles = [31, 0, 1, 2, 3, 4]
        run_tiles(tb1, na_tiles, HTN)

        if stop == 1:
            P.enabled = False
        WB = [View(arena, (56 + 8 * i) * KIB, BF16, [16, 256]) for i in range(2)]
        NAQ = View(arena, 72 * KIB, BF16, [8, TOK])
        NAK = View(arena, 88 * KIB, BF16, [8, 1536])
        NAV = View(arena, 112 * KIB, BF16, [12, 8, 129])
        assert 112 * KIB + NAV.nbytes <= 140 * KIB
        P.op("pool", lambda e: e.memset(NAV.ap[:, :, :, 128:129], 1.0), writes=NAV.k())
        pbank = [0]
        for g in range(12):
            W = WB[g % 2]
            P.dma("pool", W.ap, DR["wna"][:, :, g * 256:(g + 1) * 256], writes=W.k())
            kind = g // 4
            h0 = (g % 4) * 2
            if kind in (0, 1):
                tiles = range(1, 5) if kind == 0 else range(6)
                for j in tiles:
                    for hh in range(2):
                        b = 6 + (pbank[0] % 2)
                        pbank[0] += 1
                        for kc in range(16):
                            P.op("pe", lambda e, kc=kc, hh=hh, j=j, b=b, W=W: e.matmul(
                                psum[:, b, 0:256], lhsT=W.ap[:, kc, hh * 128:(hh + 1) * 128], rhs=HTN[j].ap[:, kc, :],
                                start=(kc == 0), stop=(kc == 15)),
                                reads=W.k() + HTN[j].r(kc * 256, kc * 256 + 256), writes=pk(b, 0, 256))
                        if kind == 0:
                            dst = NAQ
                            lo = (h0 + hh) * TOK + (j - 1) * 256
                            oap = NAQ.ap[:, h0 + hh, (j - 1) * 256:j * 256]
                        else:
                            dst = NAK
                            lo = (h0 + hh) * 1536 + j * 256
                            oap = NAK.ap[:, h0 + hh, j * 256:(j + 1) * 256]
                        evac(oap, psum[:, b, 0:256], pk(b, 0, 256), dst.r(lo, lo + 256))
            else:
                for j in range(6):
                    for s in range(2):
                        b = 6 + (pbank[0] % 2)
                        pbank[0] += 1
                        for kc in range(16):
                            P.op("pe", lambda e, kc=kc, s=s, j=j, b=b, W=W: e.matmul(
                                psum[:, b, 0:256], lhsT=HTN[j].ap[:, kc, s * 128:(s + 1) * 128], rhs=W.ap[:, kc, :],
                                start=(kc == 0), stop=(kc == 15)),
                                reads=W.k() + HTN[j].r(kc * 256, kc * 256 + 256), writes=pk(b, 0, 256))
                        ck = j * 2 + s
                        lo = (ck * 8 + h0) * 129
                        evac(NAV.ap[:, ck, h0:h0 + 2, 0:128], psum[:, b, 0:256].rearrange("p (h d) -> p h d", h=2),
                             pk(b, 0, 256), NAV.r(lo, lo + 2 * 129))

        if stop == 3:
            P.enabled = False
        NBIAS = [View(arena, (140 + 15 * i) * KIB, F32, [5, 6, 128]) for i in range(2)]
        NOUT = View(arena, 170 * KIB, F32, [8, 1024])
        STB = [View(arena, (56 + 3 * i) * KIB, F32, [6, 128]) for i in range(2)]
        PTB = [View(arena, 62 * KIB + 1536 * i, BF16, [6, 128]) for i in range(2)]
        RCP = [View(arena, int(65 * KIB) + 512 * i, F32, [1]) for i in range(2)]
        slot_of_block = [0, 1, 2, 2, 2, 2, 3, 4]
        sb_of_block = [0, 0, 2, 4, 6, 8, 10, 12]
        na_scale = 128 ** -0.5
        def na_scores(it):
            h, blk = divmod(it, 8)
            bS = 0 + 2 * (it % 2)
            tok0 = sb_of_block[blk] * 64
            for j in range(6):
                bb = bS + (j // 4)
                cc = (j % 4) * 128
                P.op("pe", lambda e, j=j, bb=bb, cc=cc, h=h, blk=blk, tok0=tok0: e.matmul(
                    psum[:, bb, cc:cc + 128], lhsT=NAK.ap[:, h, tok0 + j * 128:tok0 + (j + 1) * 128],
                    rhs=NAQ.ap[:, h, blk * 128:(blk + 1) * 128], start=True, stop=True),
                    reads=NAK.r(h * 1536 + tok0 + j * 128, h * 1536 + tok0 + (j + 1) * 128)
                    + NAQ.r(h * TOK + blk * 128, h * TOK + (blk + 1) * 128),
                    writes=pk(bb, cc, cc + 128))

        def na_rest(it):
            h, blk = divmod(it, 8)
            NB = NBIAS[h % 2]
            STt = STB[it % 2]
            PTt = PTB[it % 2]
            RC = RCP[it % 2]
            bS = 0 + 2 * (it % 2)
            bO = 4 + (it % 2)
            slot = slot_of_block[blk]
            P.op("dve", lambda e: e.scalar_tensor_tensor(
                out=STt.ap[:, 0:4, :], in0=psum[:, bS, :].rearrange("p (c q) -> p c q", c=4), scalar=na_scale,
                in1=NB.ap[:, slot, 0:4, :], op0=ALU.mult, op1=ALU.add),
                reads=pk(bS) + NB.k(), writes=STt.r(0, 512))
            P.op("dve", lambda e: e.scalar_tensor_tensor(
                out=STt.ap[:, 4:6, :], in0=psum[:, bS + 1, 0:256].rearrange("p (c q) -> p c q", c=2), scalar=na_scale,
                in1=NB.ap[:, slot, 4:6, :], op0=ALU.mult, op1=ALU.add),
                reads=pk(bS + 1, 0, 256) + NB.k(), writes=STt.r(512, 768))
            P.op("act", lambda e: e.activation(out=PTt.ap, in_=STt.ap, func=AF.Exp), reads=STt.k(), writes=PTt.k())
            for j in range(6):
                ck = sb_of_block[blk] // 2 + j
                P.op("pe", lambda e, j=j, ck=ck: e.matmul(
                    psum[:, bO, 0:129], lhsT=PTt.ap[:, j, :], rhs=NAV.ap[:, ck, h, :], start=(j == 0), stop=(j == 5)),
                    reads=PTt.k() + NAV.r((ck * 8 + h) * 129, (ck * 8 + h + 1) * 129), writes=pk(bO, 0, 129))
            P.op("dve", lambda e: e.reciprocal(out=RC.ap, in_=psum[:, bO, 128:129]), reads=pk(bO, 0, 129), writes=RC.k())
            P.op("dve", lambda e: e.tensor_scalar_mul(
                out=NOUT.ap[:, blk, h * 128:(h + 1) * 128], in0=psum[:, bO, 0:128], scalar1=RC.ap),
                reads=pk(bO, 0, 129) + RC.k(), writes=NOUT.r(blk * 1024 + h * 128, blk * 1024 + (h + 1) * 128))

        na_scores(0)
        for it in range(64):
            h, blk = divmod(it, 8)
            if blk == 0:
                NB = NBIAS[h % 2]
                P.dma("sp", NB.ap, DR["nab"][h].rearrange("p (s c q) -> p s c q", s=5, c=6), writes=NB.k())
            if it + 1 < 64:
                na_scores(it + 1)
            na_rest(it)

        if debug:
            P.dma("sp", dbg["d_nout"], NOUT.ap.rearrange("p a b -> p (a b)"), reads=NOUT.k(), writes=["d_nout"], force=True)

        def norm_to_mixT(SRC, gcol, mix_base, bump_base_kib):
            b = Bump(arena, bump_base_kib)
            GB = b.take(F32, [1024])
            NN = [b.take(BF16, [1024]) for _ in range(2)]
            JUNK = b.take(BF16, [1024])
            SS = [b.take(F32, [1]) for _ in range(2)]
            P.dma("sp", GB.ap, DR["gbc"][:, gcol:gcol + 1024], writes=GB.k())
            for tt in range(8):
                ss = SS[tt % 2]
                nn = NN[tt % 2]
                P.op("act", lambda e, tt=tt, ss=ss: e.activation(out=JUNK.ap, in_=SRC.ap[:, tt, :], func=AF.Square, accum_out=ss.ap),
                     reads=SRC.r(tt * 1024, tt * 1024 + 1024), writes=JUNK.k() + ss.k())
                rstd_from_ss(ss, ss.ap, ss.k(), 1.0 / 1024)
                P.op("dve", lambda e, tt=tt, ss=ss, nn=nn: e.scalar_tensor_tensor(
                    out=nn.ap, in0=SRC.ap[:, tt, :], scalar=ss.ap, in1=GB.ap, op0=ALU.mult, op1=ALU.mult),
                    reads=SRC.r(tt * 1024, tt * 1024 + 1024) + ss.k() + GB.k(), writes=nn.k())
                bT = 6 + (tt % 2)
                pT = psum[:, bT, :].bitcast(BF16)
                for c8 in range(8):
                    P.op("pe", lambda e, c8=c8, nn=nn, pT=pT: e.transpose(pT[:, c8 * 128:(c8 + 1) * 128], nn.ap[:, c8 * 128:(c8 + 1) * 128], IDENT.ap),
                         reads=nn.k() + IDENT.k(), writes=pk(bT))
                evac(MIXT.ap[:, mix_base:mix_base + 8, tt * 128:(tt + 1) * 128], pT.rearrange("p (c t) -> p c t", c=8),
                     pk(bT), sum([MIXT.r((mix_base + c8) * TOK + tt * 128, (mix_base + c8) * TOK + (tt + 1) * 128) for c8 in range(8)], []))

        norm_to_mixT(NOUT, 1024, 8, 88)

        if stop == 4:
            P.enabled = False
        tbq = alloc_tile_bufs(88, 1, 1, 1, kv=False)
        bq = Bump(arena, 0)
        bq.o = tbq["end"]
        WCQ = bq.take(BF16, [16, 512])
        CQN = [bq.take(BF16, [4, 256]) for _ in range(2)]
        SQ3 = [bq.take(BF16, [4, 256]) for _ in range(1)]
        RQ = [bq.take(F32, [256]) for _ in range(1)]
        WUQ = bq.take(BF16, [4, 1536])
        WUQS = bq.take(BF16, [4, 512])
        CSQ = [bq.take(F32, [2, 256]) for _ in range(2)]
        TQ1 = bq.take(F32, [256])
        TQ2 = bq.take(F32, [256])
        assert bq.o <= 172 * KIB, bq.o
        P.op("pool", lambda e: e.memset(QR.ap[64:128], 0.0), writes=QR.k())
        P.dma("pool", WCQ.ap, DR["wcq"], writes=WCQ.k())
        P.dma("pool", WUQ.ap, DR["wuq"], writes=WUQ.k())
        P.dma("pool", WUQS.ap, DR["wuqs"], writes=WUQS.k())
        for ti in range(4):
            if stop == 4.4 and ti == 1:
                P.enabled = False
            ht = tbq["HT"][0]
            proc_tile(tbq, ti, ht, False)
            t0 = ti * 256
            cqn = CQN[ti % 2]
            rq = RQ[0]
            csq = CSQ[ti % 2]
            sq3 = SQ3[0]
            P.dma("sp", csq.ap[0:64], DR["cs"][:, :, t0:t0 + 256], writes=csq.k())
            for m in range(4):
                bb = 2 + m // 2
                for kc in range(16):
                    P.op("pe", lambda e, kc=kc, m=m, bb=bb, ht=ht: e.matmul(
                        psum[:, bb, (m % 2) * 256:(m % 2 + 1) * 256], lhsT=WCQ.ap[:, kc, m * 128:(m + 1) * 128], rhs=ht.ap[:, kc, :],
                        start=(kc == 0), stop=(kc == 15)),
                        reads=WCQ.k() + ht.r(kc * 256, kc * 256 + 256), writes=pk(bb, (m % 2) * 256, (m % 2) * 256 + 256))
            for hb in range(2):
                P.op("act", lambda e, hb=hb: e.activation(out=sq3.ap[:, 2 * hb:2 * hb + 2, :],
                                                         in_=psum[:, 2 + hb, :].rearrange("p (m t) -> p m t", m=2), func=AF.Square),
                     reads=pk(2 + hb), writes=sq3.r(hb * 512, hb * 512 + 512))
            bA = 4 + (ti % 2)
            for m in range(4):
                P.op("pe", lambda e, m=m, bA=bA: e.matmul(psum[:, bA, 0:256], lhsT=ONES.ap, rhs=sq3.ap[:, m, :], start=(m == 0), stop=(m == 3)),
                     reads=ONES.k() + sq3.k(), writes=pk(bA, 0, 256))
            rstd_from_ss(rq, psum[:, bA, 0:256], pk(bA, 0, 256), 1.0 / 512)
            for m in range(4):
                bb = 2 + m // 2
                P.op("dve", lambda e, m=m, bb=bb, cqn=cqn, rq=rq: e.scalar_tensor_tensor(
                    out=cqn.ap[:, m, :], in0=psum[:, bb, (m % 2) * 256:(m % 2 + 1) * 256], scalar=GV.ap[:, 16 + m:17 + m], in1=rq.ap,
                    op0=ALU.mult, op1=ALU.mult),
                    reads=pk(bb, (m % 2) * 256, (m % 2) * 256 + 256) + GV.k() + rq.k(), writes=cqn.r(m * 256, m * 256 + 256))
            if stop == 4.1:
                P.enabled = False
            for h in range(8):
                if stop == 4.3 and h == 1:
                    P.enabled = False
                bN = h % 2
                bR = 6 + (h % 2)
                for kc in range(4):
                    P.op("pe", lambda e, kc=kc, h=h, cqn=cqn, bN=bN: e.matmul(
                        psum[:, bN, 0:256], lhsT=WUQ.ap[:, kc, h * 192:h * 192 + 128], rhs=cqn.ap[:, kc, :],
                        start=(kc == 0), stop=(kc == 3)),
                        reads=WUQ.k() + cqn.k(), writes=pk(bN, (h % 2) * 256, (h % 2) * 256 + 256))
                evac(QN.ap[:, h, t0:t0 + 256], psum[:, bN, 0:256], pk(bN),
                     QN.r(h * TOK + t0, h * TOK + t0 + 256))
                if stop == 4.2:
                    P.enabled = False
                for kc in range(4):
                    P.op("pe", lambda e, kc=kc, h=h, cqn=cqn, bR=bR: e.matmul(
                        psum[0:64, bR, 0:256], lhsT=WUQ.ap[:, kc, h * 192 + 128:h * 192 + 192], rhs=cqn.ap[:, kc, :],
                        start=(kc == 0), stop=(kc == 3)),
                        reads=WUQ.k() + cqn.k(), writes=pk(bR, 0, 256))
                for kc in range(4):
                    P.op("pe", lambda e, kc=kc, h=h, cqn=cqn, bR=bR: e.matmul(
                        psum[0:64, bR, 256:512], lhsT=WUQS.ap[:, kc, h * 64:(h + 1) * 64], rhs=cqn.ap[:, kc, :],
                        start=(kc == 0), stop=(kc == 3)),
                        reads=WUQS.k() + cqn.k(), writes=pk(bR, 256, 512))
                P.op("dve", lambda e, csq=csq, bR=bR: e.tensor_tensor(out=TQ1.ap[0:64], in0=psum[0:64, bR, 0:256], in1=csq.ap[0:64, 0, :], op=ALU.mult),
                     reads=pk(bR, 0, 256) + csq.k(), writes=TQ1.k())
                P.op("dve", lambda e, csq=csq, bR=bR: e.tensor_tensor(out=TQ2.ap[0:64], in0=psum[0:64, bR, 256:512], in1=csq.ap[0:64, 1, :], op=ALU.mult),
                     reads=pk(bR, 256, 512) + csq.k(), writes=TQ2.k())
                P.op("pool", lambda e, h=h, t0=t0: e.tensor_tensor(out=QR.ap[0:64, h, t0:t0 + 256], in0=TQ1.ap[0:64], in1=TQ2.ap[0:64], op=ALU.add),
                     reads=TQ1.k() + TQ2.k(), writes=QR.r(h * TOK + t0, h * TOK + t0 + 256))

        if stop == 4.5:
            P.enabled = False
        tb5 = alloc_tile_bufs(88, 2, 1, 2)
        tb5["XT"].append(View(arena, 56 * KIB, F32, [16, 256]))
        assert tb5["end"] <= 172 * KIB, tb5["end"]
        P.dma("pool", tb5["WA"].ap, DR["wa"], writes=tb5["WA"].k())
        run_tiles(tb5, list(range(5, 31)), [tb5["HT"][i % 2] for i in range(26)])

        if debug:
            P.dma("sp", dbg["d_ckv"], CKV.ap.rearrange("p a b -> p (a b)"), reads=CKV.k(), writes=["d_ckv"], force=True)
            P.dma("sp", dbg["d_krt"], KRT.ap[0:64], reads=KRT.k(), writes=["d_krt"], force=True)
            P.dma("sp", dbg["d_qn"], QN.ap.rearrange("p a b -> p (a b)"), reads=QN.k(), writes=["d_qn"], force=True)
            P.dma("sp", dbg["d_qr"], QR.ap[0:64].rearrange("p a b -> p (a b)"), reads=QR.k(), writes=["d_qr"], force=True)

        if stop == 5:
            P.enabled = False
        bm = Bump(arena, 88)
        KB = [bm.take(BF16, [2048]) for _ in range(2)]
        VB = [bm.take(BF16, [16, 129]) for _ in range(2)]
        WUKV = bm.take(BF16, [2, 2048])
        AOUT = bm.take(F32, [8, 1024])
        PT = [bm.take(BF16, [512]) for _ in range(4)]
        RCM = [bm.take(F32, [1]) for _ in range(2)]
        assert bm.o <= 172 * KIB, bm.o
        P.dma("pool", WUKV.ap, DR["wukv"], writes=WUKV.k())
        WO = [View(arena, (152 + 16 * i) * KIB, BF16, [16, 512]) for i in range(2)]
        P.dma("pool", WO[0].ap, DR["wout"][:, :, 0:512], writes=WO[0].k())
        for v in VB:
            P.op("pool", lambda e, v=v: e.memset(v.ap[:, :, 128:129], 1.0), writes=v.k())
        mla_scale = 192 ** -0.5

        def okey(qt):
            return pk(qt // 3, (qt % 3) * 160, (qt % 3) * 160 + 129)

        def oap(qt, lo, hi):
            return psum[:, qt // 3, (qt % 3) * 160 + lo:(qt % 3) * 160 + hi]

        def gen_kv(n):
            h, kb = divmod(n, 4)
            kbuf = KB[n % 2]
            vbuf = VB[n % 2]
            k0 = kb * 2048
            for t4 in range(4):
                b = (6, 7)[t4 % 2]
                for c in range(2):
                    P.op("pe", lambda e, c=c, t4=t4, b=b, h=h, k0=k0: e.matmul(
                        psum[:, b, :], lhsT=WUKV.ap[:, c, h * 256:h * 256 + 128], rhs=CKV.ap[:, c, k0 + t4 * 512:k0 + (t4 + 1) * 512],
                        start=(c == 0), stop=(c == 1)),
                        reads=WUKV.k() + CKV.r(c * S + k0 + t4 * 512, c * S + k0 + (t4 + 1) * 512), writes=pk(b))
                evac(kbuf.ap[:, t4 * 512:(t4 + 1) * 512], psum[:, b, :], pk(b), kbuf.r(t4 * 512, (t4 + 1) * 512), eng="dve")
            for j4 in range(4):
                b = (6, 7)[j4 % 2]
                for jj in range(4):
                    j = j4 * 4 + jj
                    for c in range(2):
                        P.op("pe", lambda e, c=c, j=j, jj=jj, h=h, k0=k0, b=b: e.matmul(
                            psum[:, b, jj * 128:(jj + 1) * 128], lhsT=CKV.ap[:, c, k0 + j * 128:k0 + (j + 1) * 128],
                            rhs=WUKV.ap[:, c, h * 256 + 128:h * 256 + 256], start=(c == 0), stop=(c == 1)),
                            reads=WUKV.k() + CKV.r(c * S + k0 + j * 128, c * S + k0 + (j + 1) * 128), writes=pk(b, jj * 128, (jj + 1) * 128))
                evac(vbuf.ap[:, j4 * 4:(j4 + 1) * 4, 0:128], psum[:, b, :].rearrange("p (j d) -> p j d", j=4),
                     pk(b), vbuf.r(j4 * 4 * 129, (j4 + 1) * 4 * 129), eng="dve")

        sc_ctr = [0]

        def attn_block(n):
            h, kb = divmod(n, 4)
            kbuf = KB[n % 2]
            vbuf = VB[n % 2]
            k0 = kb * 2048
            steps = [(qg, j) for qg in range(2) for j in range(16)]

            def emit_scores(qg, j, i):
                bS = 3 + (i % 3)
                P.op("pe", lambda e, j=j, qg=qg, bS=bS: e.matmul(
                    psum[:, bS, :], lhsT=kbuf.ap[:, j * 128:(j + 1) * 128], rhs=QN.ap[:, h, qg * 512:(qg + 1) * 512], start=True, stop=False),
                    reads=kbuf.r(j * 128, (j + 1) * 128) + QN.r(h * TOK + qg * 512, h * TOK + (qg + 1) * 512), writes=pk(bS))
                P.op("pe", lambda e, j=j, qg=qg, bS=bS: e.matmul(
                    psum[:, bS, :], lhsT=KRT.ap[:, k0 + j * 128:k0 + (j + 1) * 128], rhs=QR.ap[:, h, qg * 512:(qg + 1) * 512],
                    start=False, stop=True),
                    reads=KRT.r(k0 + j * 128, k0 + (j + 1) * 128) + QR.r(h * TOK + qg * 512, h * TOK + (qg + 1) * 512), writes=pk(bS))

            base = sc_ctr[0]
            emit_scores(steps[0][0], steps[0][1], base)
            emit_scores(steps[1][0], steps[1][1], base + 1)
            for si, (qg, j) in enumerate(steps):
                i = base + si
                bS = 3 + (i % 3)
                pt = PT[i % 4]
                if si + 2 < len(steps):
                    emit_scores(steps[si + 2][0], steps[si + 2][1], i + 2)
                P.op("act", lambda e, bS=bS, pt=pt: e.activation(out=pt.ap, in_=psum[:, bS, :], func=AF.Exp, scale=mla_scale),
                     reads=pk(bS), writes=pt.k())
                for qs in range(4):
                    qt = qg * 4 + qs
                    P.op("pe", lambda e, qs=qs, qt=qt, j=j, pt=pt: e.matmul(
                        oap(qt, 0, 129), lhsT=pt.ap[:, qs * 128:(qs + 1) * 128], rhs=vbuf.ap[:, j, :],
                        start=(kb == 0 and j == 0 and qt % 3 == 0), stop=(kb == 3 and j == 15), skip_group_check=True),
                        reads=pt.k() + vbuf.r(j * 129, (j + 1) * 129), writes=okey(qt))
            sc_ctr[0] += len(steps)
            if kb == 3:
                for qt in range(8):
                    rc = RCM[qt % 2]
                    P.op("dve", lambda e, qt=qt, rc=rc: e.reciprocal(out=rc.ap, in_=oap(qt, 128, 129)), reads=okey(qt), writes=rc.k())
                    P.op("dve", lambda e, qt=qt, rc=rc: e.tensor_scalar_mul(
                        out=AOUT.ap[:, qt, h * 128:(h + 1) * 128], in0=oap(qt, 0, 128), scalar1=rc.ap),
                        reads=okey(qt) + rc.k(), writes=AOUT.r(qt * 1024 + h * 128, qt * 1024 + (h + 1) * 128))

        gen_kv(0)
        for n in range(32):
            if n + 1 < 32:
                gen_kv(n + 1)
            attn_block(n)

        if debug:
            P.dma("sp", dbg["d_aout"], AOUT.ap.rearrange("p a b -> p (a b)"), reads=AOUT.k(), writes=["d_aout"], force=True)

        norm_to_mixT(AOUT, 0, 0, 188)

        if stop == 6:
            P.enabled = False
        X1 = View(arena, 88 * KIB, F32, [8, D])
        W1 = [View(arena, (8 + 16 * i) * KIB, BF16, [16, 512]) for i in range(2)]
        W2 = [View(arena, (40 + 16 * i) * KIB, BF16, [4, D]) for i in range(2)]
        P.dma("pool", WO[1].ap, DR["wout"][:, :, 512:1024], writes=WO[1].k())
        for tt in range(8):
            P.dma("sp", X1.ap[:, tt, :], DR["xown"][tt * 128:(tt + 1) * 128, :], writes=X1.r(tt * D, (tt + 1) * D))
        pb7 = [0]
        for cg in range(4):
            W = WO[cg % 2]
            if cg >= 2:
                P.dma("pool", W.ap, DR["wout"][:, :, cg * 512:(cg + 1) * 512], writes=W.k())
            if cg == 3:
                P.dma("pool", W1[0].ap, DR["wff1"][:, :, 0:512], writes=W1[0].k())
                P.dma("pool", W2[0].ap, DR["wff2"][:, 0:4, :], writes=W2[0].k())
                P.dma("pool", W1[1].ap, DR["wff1"][:, :, 512:1024], writes=W1[1].k())
            for tt in range(8):
                b = pb7[0] % 4
                pb7[0] += 1
                for kc in range(16):
                    P.op("pe", lambda e, kc=kc, tt=tt, b=b, W=W: e.matmul(
                        psum[:, b, :], lhsT=MIXT.ap[:, kc, tt * 128:(tt + 1) * 128], rhs=W.ap[:, kc, :], start=(kc == 0), stop=(kc == 15)),
                        reads=MIXT.r(kc * TOK + tt * 128, kc * TOK + (tt + 1) * 128) + W.k(), writes=pk(b))
                lo = tt * D + cg * 512
                P.op("dve", lambda e, tt=tt, cg=cg, b=b: e.tensor_tensor(
                    out=X1.ap[:, tt, cg * 512:(cg + 1) * 512], in0=psum[:, b, :], in1=X1.ap[:, tt, cg * 512:(cg + 1) * 512], op=ALU.add),
                    reads=pk(b) + X1.r(lo, lo + 512), writes=X1.r(lo, lo + 512))
        if debug:
            P.dma("sp", dbg["d_x1"], X1.ap.rearrange("p a b -> p (a b)"), reads=X1.k(), writes=["d_x1"], force=True)

        H2T = View(arena, 152 * KIB, BF16, [16, TOK])
        b7 = Bump(arena, 188)
        GM = b7.take(F32, [D])
        H2 = [b7.take(BF16, [D]) for _ in range(2)]
        assert b7.o <= 204 * KIB
        JK = View(arena, 184 * KIB, BF16, [D])
        P.dma("sp", GM.ap, DR["gbc"][:, 2048:4096], writes=GM.k())
        for tt in range(8):
            ss = SS7[tt % 2]
            h2 = H2[tt % 2]
            P.op("act", lambda e, tt=tt, ss=ss: e.activation(out=JK.ap, in_=X1.ap[:, tt, :], func=AF.Square, accum_out=ss.ap),
                 reads=X1.r(tt * D, (tt + 1) * D), writes=JK.k() + ss.k())
            rstd_from_ss(ss, ss.ap, ss.k(), 1.0 / D)
            P.op("dve", lambda e, tt=tt, ss=ss, h2=h2: e.scalar_tensor_tensor(
                out=h2.ap, in0=X1.ap[:, tt, :], scalar=ss.ap, in1=GM.ap, op0=ALU.mult, op1=ALU.mult),
                reads=X1.r(tt * D, (tt + 1) * D) + ss.k() + GM.k(), writes=h2.k())
            for half in range(2):
                bT = 4 + ((2 * tt + half) % 4)
                pT = psum[:, bT, :].bitcast(BF16)
                for c8 in range(8):
                    cc = half * 8 + c8
                    P.op("pe", lambda e, c8=c8, cc=cc, h2=h2, pT=pT: e.transpose(pT[:, c8 * 128:(c8 + 1) * 128], h2.ap[:, cc * 128:(cc + 1) * 128], IDENT.ap),
                         reads=h2.k() + IDENT.k(), writes=pk(bT))
                evac(H2T.ap[:, half * 8:(half + 1) * 8, tt * 128:(tt + 1) * 128], pT.rearrange("p (c t) -> p c t", c=8), pk(bT),
                     sum([H2T.r((half * 8 + c8) * TOK + tt * 128, (half * 8 + c8) * TOK + (tt + 1) * 128) for c8 in range(8)], []))

        if stop == 7:
            P.enabled = False
        AT = [View(arena, (72 + 8 * i) * KIB, BF16, [4, TOK]) for i in range(2)]
        RT = [View(arena, (184 + 2 * i) * KIB, F32, [512]) for i in range(2)]
        rt_ctr = [0]
        pf = [0]
        for g in range(16):
            w1 = W1[g % 2]
            w2 = W2[g % 2]
            at = AT[g % 2]
            if g >= 2:
                P.dma("pool", w1.ap, DR["wff1"][:, :, g * 512:(g + 1) * 512], writes=w1.k())
            if g >= 1:
                P.dma("pool", w2.ap, DR["wff2"][:, g * 4:(g + 1) * 4, :], writes=w2.k())
            for jc in range(4):
                for th in range(2):
                    b = pf[0] % 4
                    pf[0] += 1
                    for kc in range(16):
                        P.op("pe", lambda e, kc=kc, jc=jc, th=th, b=b, w1=w1: e.matmul(
                            psum[:, b, :], lhsT=w1.ap[:, kc, jc * 128:(jc + 1) * 128], rhs=H2T.ap[:, kc, th * 512:(th + 1) * 512],
                            start=(kc == 0), stop=(kc == 15)),
                            reads=w1.k() + H2T.r(kc * TOK + th * 512, kc * TOK + (th + 1) * 512), writes=pk(b))
                    rt = RT[rt_ctr[0] % 2]
                    rt_ctr[0] += 1
                    P.op("act", lambda e, b=b, rt=rt: e.activation(out=rt.ap, in_=psum[:, b, :], func=AF.Relu), reads=pk(b), writes=rt.k())
                    P.op("pool", lambda e, rt=rt, at=at, jc=jc, th=th: e.tensor_tensor(
                        out=at.ap[:, jc, th * 512:(th + 1) * 512], in0=rt.ap, in1=rt.ap, op=ALU.mult),
                        reads=rt.k(), writes=at.r(jc * TOK + th * 512, jc * TOK + (th + 1) * 512))
            for tt in range(8):
                for cg in range(4):
                    b = 4 + (pf[0] % 4)
                    pf[0] += 1
                    for jc in range(4):
                        P.op("pe", lambda e, jc=jc, tt=tt, cg=cg, b=b, at=at, w2=w2: e.matmul(
                            psum[:, b, :], lhsT=at.ap[:, jc, tt * 128:(tt + 1) * 128], rhs=w2.ap[:, jc, cg * 512:(cg + 1) * 512],
                            start=(jc == 0), stop=(jc == 3)),
                            reads=at.r(jc * TOK + tt * 128, jc * TOK + (tt + 1) * 128) + w2.r(jc * D + cg * 512, jc * D + (cg + 1) * 512),
                            writes=pk(b))
                    lo = tt * D + cg * 512
                    P.op("dve", lambda e, tt=tt, cg=cg, b=b: e.tensor_tensor(
                        out=X1.ap[:, tt, cg * 512:(cg + 1) * 512], in0=psum[:, b, :], in1=X1.ap[:, tt, cg * 512:(cg + 1) * 512], op=ALU.add),
                        reads=pk(b) + X1.r(lo, lo + 512), writes=X1.r(lo, lo + 512))

        b9 = Bump(arena, 8)
        GF = b9.take(F32, [D])
        OT = [b9.take(F32, [D]) for _ in range(2)]
        JK9 = b9.take(BF16, [D])
        SS9 = [b9.take(F32, [1]) for _ in range(2)]
        assert b9.o <= 40 * KIB
        P.dma("sp", GF.ap, DR["gbc"][:, 4096:6144], writes=GF.k())
        outs = []
        for tt in range(8):
            ss = SS9[tt % 2]
            ot = OT[tt % 2]
            P.op("act", lambda e, tt=tt, ss=ss: e.activation(out=JK9.ap, in_=X1.ap[:, tt, :], func=AF.Square, accum_out=ss.ap),
                 reads=X1.r(tt * D, (tt + 1) * D), writes=JK9.k() + ss.k())
            rstd_from_ss(ss, ss.ap, ss.k(), 1.0 / D)
            P.op("dve", lambda e, tt=tt, ss=ss, ot=ot: e.scalar_tensor_tensor(
                out=ot.ap, in0=X1.ap[:, tt, :], scalar=ss.ap, in1=GF.ap, op0=ALU.mult, op1=ALU.mult),
                reads=X1.r(tt * D, (tt + 1) * D) + ss.k() + GF.k(), writes=ot.k())
            P.dma("sp", y[tt * 128:(tt + 1) * 128, :], ot.ap, reads=ot.k(), writes=[("y", tt)])
            outs.append(("y", tt))
        P.fence("sp", outs + (list(dbg.keys()) if debug else []))
        P.finish(st)
        build_program.stats = P.stats
    return nc


def _pk(w, kchunks):
    kp, c = w.shape
    return np.ascontiguousarray(w.reshape(kchunks, 128, c).transpose(1, 0, 2))


def _swap_halves(w, nheads, width):
    r = w.reshape(w.shape[0], nheads, 2, width // 2)
    return np.ascontiguousarray(r[:, :, ::-1, :]).reshape(w.shape[0], nheads * width)


def _na_bias_tables(rpb, core):
    H = 8
    ROWS, W, KR, KC = 128, 64, 8, 16
    out = np.full((H, 5, 768, 128), NEG, dtype=np.float32)
    blocks_of_slot = {0: 0, 1: 1, 2: 3, 3: 6, 4: 7}
    sb_of_block = [0, 0, 2, 4, 6, 8, 10, 12]
    col = np.arange(W)
    cstart = np.clip(col - KC // 2, 0, W - KC)
    for slot, blk in blocks_of_slot.items():
        for dq in range(2):
            r = core * 16 + 2 * blk + dq
            rs = int(np.clip(r - KR // 2, 0, ROWS - KR))
            for i in range(KR):
                rk = rs + i
                lr = rk - core * 16 + 4 - sb_of_block[blk]
                assert 0 <= lr < 12
                dr = rk - r + (KR - 1)
                for c in range(W):
                    q = dq * 64 + c
                    ck = cstart[c] + np.arange(KC)
                    dc = ck - c + (KC - 1)
                    out[:, slot, lr * 64 + ck, q] = rpb[:, dr, dc]
    out = out.reshape(H, 5, 6, 128, 128).transpose(0, 3, 1, 2, 4)
    return np.ascontiguousarray(out).reshape(H, 128, 5 * 6 * 128)


_CACHE = {}


def kernel(x, attn_norm_g, w_in, q_norm_g, w_uq, kv_norm_g, w_ukv, na_rpb, mla_out_norm_g, na_out_norm_g,
           w_out, mlp_norm_g, w_ff1, w_ff2, final_norm_g, _debug=False, _stop=None, _cores=None):
    f = np.float32
    x = np.asarray(x, f)[0]
    w_in = np.asarray(w_in, f)[0]
    w_uq = np.asarray(w_uq, f)[0]
    w_ukv = np.asarray(w_ukv, f)[0]
    w_out = np.asarray(w_out, f)[0]
    w_ff1 = np.asarray(w_ff1, f)[0]
    w_ff2 = np.asarray(w_ff2, f)[0]
    rpb = np.asarray(na_rpb, f)[0]
    g_attn = np.asarray(attn_norm_g, f)[0]
    g_q = np.asarray(q_norm_g, f)[0]
    g_kv = np.asarray(kv_norm_g, f)[0]
    g_mla = np.asarray(mla_out_norm_g, f)[0]
    g_na = np.asarray(na_out_norm_g, f)[0]
    g_mlp = np.asarray(mlp_norm_g, f)[0]
    g_fin = np.asarray(final_norm_g, f)

    half = 32
    inv = (10000.0 ** (-np.arange(half, dtype=f) / half)).astype(f)
    ang = (np.arange(S, dtype=f)[:, None] * inv[None, :]).astype(f)
    cosT = np.cos(ang).astype(f).T
    sinT = np.sin(ang).astype(f).T
    cs = np.stack([np.concatenate([cosT, cosT], 0), np.concatenate([-sinT, sinT], 0)], axis=1)

    xT = _pk(np.ascontiguousarray(x.T), 16)
    w_rope = w_in[:, 768:832]
    wa = _pk(np.concatenate([w_in[:, 512:832], _swap_halves(w_rope, 1, 64)], axis=1), 16)
    wcq = _pk(w_in[:, 0:512], 16)
    wna = _pk(w_in[:, 832:3904], 16)
    wuq = _pk(w_uq, 4)
    uq_rope = w_uq.reshape(512, 8, 192)[:, :, 128:192].reshape(512, 512)
    wuqs = _pk(_swap_halves(uq_rope, 8, 64), 4)
    wukv = _pk(w_ukv, 2)
    wout = _pk(w_out, 16)
    wff1 = _pk(w_ff1, 16)
    wff2 = _pk(w_ff2, 64)
    gvec = np.concatenate([g_attn.reshape(16, 128).T, g_q.reshape(4, 128).T, g_kv.reshape(2, 128).T], axis=1)
    gvec = np.ascontiguousarray(gvec, dtype=f)
    gbc = np.ascontiguousarray(np.broadcast_to(np.concatenate([g_mla, g_na, g_mlp, g_fin])[None, :], (128, 6144)), dtype=f)

    ident = np.eye(128, dtype=f)
    key = (bool(_debug), _stop)
    if key not in _CACHE:
        _CACHE[key] = build_program(debug=_debug, stop=_stop)
    nc = _CACHE[key]

    in_maps = []
    cores = list(range(NCORES)) if _cores is None else list(_cores)
    for c in cores:
        sh = c * TOK
        in_maps.append(dict(
            xT=np.ascontiguousarray(np.roll(xT, -sh, axis=2)),
            xown=np.ascontiguousarray(x[sh:sh + TOK]),
            cs=np.ascontiguousarray(np.roll(cs, -sh, axis=2)),
            wa=wa, wcq=wcq, wna=wna, wuq=wuq, wuqs=wuqs, wukv=wukv, wout=wout, wff1=wff1, wff2=wff2,
            gvec=gvec, gbc=gbc, nab=_na_bias_tables(rpb, c), ident=ident,
        ))
    res = run_bass_kernel_spmd(nc, in_maps, core_ids=list(range(len(cores))))
    if _debug:
        kernel.last = res
        if _cores is not None:
            return None
    out = np.concatenate([np.asarray(r["y"], dtype=f) for r in res.results], axis=0)
    return out.reshape(1, S, D)
```

```python
import contextlib
import numpy as np
import concourse.bass as bass
import concourse.mybir as mybir
from concourse.bass_utils import run_bass_kernel_spmd

F32 = mybir.dt.float32
BF16 = mybir.dt.bfloat16
ALU = mybir.AluOpType
AF = mybir.ActivationFunctionType

NCORES = 8
S = 8192
D = 2048
TOK = S // NCORES
EPS = 1e-6
KIB = 1024
ARENA = 204 * KIB
CH = 512
NEG = -30000.0


class Prog:
    ENGS = ("pe", "act", "dve", "pool", "sp")
    NDMA = {"sp": 8, "pool": 8, "act": 4}

    def __init__(self, nc):
        self.nc = nc
        self.ops = []
        self.enabled = True

    def op(self, eng, fn, reads=(), writes=(), dma=False, force=False):
        if not (self.enabled or force):
            return
        self.ops.append(dict(eng=eng, fn=fn, reads=tuple(reads), writes=tuple(writes), dma=dma))

    def dma(self, q, out, in_, reads=(), writes=(), force=False):
        self.op(q, lambda e: e.dma_start(out=out, in_=in_), reads, writes, dma=True, force=force)

    def fence(self, eng, reads):
        self.op(eng, None, reads, (), force=True)

    def finish(self, stack):
        nc = self.nc
        ops = self.ops
        n = len(ops)
        lw = {}
        rd = {}
        need = [None] * n
        milestone = [False] * n
        lastacc = {}
        for i, o in enumerate(ops):
            d = set()
            excl = set(k for k in o["reads"] + o["writes"] if isinstance(k, tuple) and k[0] == "ps")
            for k in excl:
                la = lastacc.setdefault(k, {})
                for e2, j in la.items():
                    if e2 != o["eng"]:
                        d.add(j)
                la[o["eng"]] = i
            for r in o["reads"]:
                if r in excl:
                    continue
                j = lw.get(r)
                if j is not None:
                    d.add(j)
            for w in o["writes"]:
                if w in excl:
                    continue
                j = lw.get(w)
                if j is not None:
                    d.add(j)
                for j in rd.get(w, ()):
                    if ops[j]["eng"] == o["eng"] and o["eng"] in ("act", "dve") and not ops[j]["dma"] and not o["dma"]:
                        continue
                    d.add(j)
            d.discard(i)
            best = {}
            for j in d:
                oj = ops[j]
                if oj["dma"]:
                    best[("dma", j)] = j
                else:
                    if oj["eng"] == "pe" and o["eng"] == "pe" and not o["dma"]:
                        continue
                    key = ("eng", oj["eng"])
                    if key not in best or best[key] < j:
                        best[key] = j
            need[i] = sorted(best.values())
            for j in need[i]:
                if not ops[j]["dma"]:
                    milestone[j] = True
            for r in o["reads"]:
                if r not in excl:
                    rd.setdefault(r, []).append(i)
            for w in o["writes"]:
                if w not in excl:
                    lw[w] = i
                    rd[w] = []
        sems = {}
        for e in ("pe", "act", "dve", "pool"):
            sems[e] = stack.enter_context(nc.semaphore("s_" + e))
        dsems = {}
        for q, k in self.NDMA.items():
            dsems[q] = [stack.enter_context(nc.semaphore("d_%s%d" % (q, t))) for t in range(k)]
        tok = [None] * n
        cnt = {e: 0 for e in sems}
        dcount = {q: 0 for q in dsems}
        dval = {q: [0] * len(dsems[q]) for q in dsems}
        for i, o in enumerate(ops):
            if o["fn"] is None:
                continue
            if o["dma"]:
                q = o["eng"]
                t = dcount[q] % len(dsems[q])
                dcount[q] += 1
                dval[q][t] += 16
                tok[i] = (dsems[q][t], dval[q][t])
            elif milestone[i]:
                cnt[o["eng"]] += 1
                tok[i] = (sems[o["eng"]], cnt[o["eng"]])
        self.stats = dict(n_ops=n, milestones=dict(cnt), dmas=dict(dcount))
        per_eng = {e: [] for e in self.ENGS}
        for i, o in enumerate(ops):
            per_eng[o["eng"]].append(i)

        def emit(engname, e):
            waited = {}
            for i in per_eng[engname]:
                o = ops[i]
                for j in need[i]:
                    s, v = tok[j]
                    key = id(s)
                    if waited.get(key, 0) >= v:
                        continue
                    waited[key] = v
                    e.wait_ge(s, v)
                if o["fn"] is None:
                    continue
                ins = o["fn"](e)
                if tok[i] is not None:
                    ins.then_inc(tok[i][0], 16 if o["dma"] else 1)

        block = stack.enter_context(nc.Block())

        @block.tensor
        def _(e):
            emit("pe", e)

        @block.scalar
        def _(e):
            emit("act", e)

        @block.vector
        def _(e):
            emit("dve", e)

        @block.gpsimd
        def _(e):
            emit("pool", e)

        @block.sync
        def _(e):
            emit("sp", e)


class View:
    def __init__(self, arena, off, dtype, shape):
        self.off = off
        self.es = 2 if dtype == BF16 else 4
        self.shape = tuple(shape)
        n = 1
        for s in shape:
            n *= s
        self.n = n
        self.nbytes = n * self.es
        assert off % 4 == 0 and off + self.nbytes <= ARENA, (off, self.nbytes)
        a = arena[:, off // 2:(off + self.nbytes) // 2]
        if dtype != BF16:
            a = a.bitcast(dtype)
        if len(shape) == 2:
            a = a.rearrange("p (a b) -> p a b", a=shape[0])
        elif len(shape) == 3:
            a = a.rearrange("p (a b c) -> p a b c", a=shape[0], b=shape[1])
        elif len(shape) == 4:
            a = a.rearrange("p (a b c d) -> p a b c d", a=shape[0], b=shape[1], c=shape[2])
        self.ap = a

    def r(self, lo, hi):
        b0 = (self.off + lo * self.es) // CH
        b1 = (self.off + hi * self.es - 1) // CH
        return list(range(b0, b1 + 1))

    def k(self):
        return self.r(0, self.n)


class Bump:
    def __init__(self, arena, base_kib):
        self.arena = arena
        self.o = int(base_kib * KIB)

    def take(self, dtype, shape):
        v = View(self.arena, self.o, dtype, shape)
        self.o += (v.nbytes + CH - 1) // CH * CH
        return v


def build_program(debug=False, stop=None):
    nc = bass.Bass("TRN2", target_bir_lowering=False)
    DR = {}

    def din(name, shape):
        DR[name] = nc.dram_tensor(name, list(shape), F32, kind="ExternalInput").ap()

    din("xT", [128, 16, S])
    din("xown", [TOK, D])
    din("cs", [64, 2, S])
    din("wa", [128, 16, 384])
    din("wcq", [128, 16, 512])
    din("wna", [128, 16, 3072])
    din("wuq", [128, 4, 1536])
    din("wuqs", [128, 4, 512])
    din("wukv", [128, 2, 2048])
    din("wout", [128, 16, 2048])
    din("wff1", [128, 16, 8192])
    din("wff2", [128, 64, 2048])
    din("gvec", [128, 22])
    din("gbc", [128, 6144])
    din("nab", [8, 128, 3840])
    din("ident", [128, 128])
    y = nc.dram_tensor("y", [TOK, D], F32, kind="ExternalOutput").ap()
    dbg = {}
    if debug:
        for name, shape in (("d_ckv", [128, 2 * S]), ("d_krt", [64, S]), ("d_nout", [128, 8 * 1024]),
                            ("d_aout", [128, 8 * 1024]), ("d_x1", [128, 8 * 2048]), ("d_qn", [128, 8 * 1024]),
                            ("d_qr", [64, 8 * 1024])):
            dbg[name] = nc.dram_tensor(name, shape, F32 if name in ("d_nout", "d_aout", "d_x1") else BF16,
                                       kind="ExternalOutput").ap()

    st = contextlib.ExitStack()
    with st:
        arena = st.enter_context(nc.sbuf_tensor("arena", [128, ARENA // 2], BF16))
        psum = st.enter_context(nc.psum_tensor("ps", [128, 8, 512], F32))
        P = Prog(nc)

        def pk(b, lo=0, hi=512):
            return [("ps", b)]

        bp = Bump(arena, 0)
        ONES = bp.take(BF16, [128])
        IDENT = bp.take(BF16, [128])
        GV = bp.take(F32, [22])
        EPSV = bp.take(F32, [1])
        SS7 = [bp.take(F32, [1]) for _ in range(2)]
        assert bp.o <= 8 * KIB
        CKV = View(arena, 8 * KIB, BF16, [2, S])
        KRT = View(arena, 40 * KIB, BF16, [S])
        MIXT = View(arena, 56 * KIB, BF16, [16, TOK])
        QN = View(arena, 172 * KIB, BF16, [8, TOK])
        QR = View(arena, 188 * KIB, BF16, [8, TOK])

        P.op("dve", lambda e: e.memset(ONES.ap, 1.0), writes=ONES.k())
        P.op("dve", lambda e: e.memset(EPSV.ap, EPS), writes=EPSV.k())
        P.op("pool", lambda e: e.memset(KRT.ap[64:128], 0.0), writes=KRT.k())
        P.dma("sp", GV.ap, DR["gvec"], writes=GV.k())
        P.dma("pool", IDENT.ap, DR["ident"], writes=IDENT.k())

        evac_rr = [0]

        def evac(out_ap, in_ap, reads, writes, eng=None):
            if eng is None:
                eng = ("act", "dve")[evac_rr[0] % 2]
                evac_rr[0] += 1
            if eng == "act":
                P.op("act", lambda e: e.activation(out=out_ap, in_=in_ap, func=AF.Copy), reads, writes)
            else:
                P.op("dve", lambda e: e.tensor_copy(out=out_ap, in_=in_ap), reads, writes)

        def rstd_from_ss(dst, src_ap, src_keys, inv_n):
            P.op("act", lambda e: e.activation(out=dst.ap, in_=src_ap, func=AF.Ln, scale=inv_n, bias=EPSV.ap[0:dst.ap.shape[0]]),
                 reads=list(src_keys) + EPSV.k(), writes=dst.k())
            P.op("act", lambda e: e.activation(out=dst.ap, in_=dst.ap, func=AF.Exp, scale=-0.5), reads=dst.k(), writes=dst.k())

        def alloc_tile_bufs(base_kib, nxt, nsq, nht, kv=True):
            b = Bump(arena, base_kib)
            tb = dict()
            tb["XT"] = [b.take(F32, [16, 256]) for _ in range(nxt)]
            tb["SQ"] = [b.take(BF16, [16, 256]) for _ in range(nsq)]
            tb["HT"] = [b.take(BF16, [16, 256]) for _ in range(nht)]
            tb["RS"] = [b.take(F32, [256]) for _ in range(2)]
            if kv:
                tb["WA"] = b.take(BF16, [16, 384])
                tb["CS"] = [b.take(F32, [2, 256]) for _ in range(2)]
                tb["R2"] = [b.take(F32, [256]) for _ in range(2)]
                tb["SQ2"] = [b.take(BF16, [2, 256]) for _ in range(2)]
                tb["T1"] = [b.take(F32, [256]) for _ in range(1)]
                tb["T2"] = [b.take(F32, [256]) for _ in range(1)]
            tb["end"] = b.o
            return tb

        tile_ctr = [0]

        def proc_tile(tb, ti, ht, do_kv):
            c = tile_ctr[0]
            tile_ctr[0] += 1
            stage_a1(tb, ti, ht, c)
            stage_a2(tb, ti, ht, c)
            if do_kv:
                stage_b_mm(tb, ti, ht, c)
                stage_b_tail(tb, ti, ht, c)

        def run_tiles(tb, tiles, hts):
            cs_ = []
            for _ in tiles:
                cs_.append(tile_ctr[0])
                tile_ctr[0] += 1
            n_ = len(tiles)
            stage_a1(tb, tiles[0], hts[0], cs_[0])
            if n_ > 1:
                stage_a1(tb, tiles[1], hts[1], cs_[1])
            stage_a2(tb, tiles[0], hts[0], cs_[0])
            for i in range(n_):
                stage_b_mm(tb, tiles[i], hts[i], cs_[i])
                if i + 2 < n_:
                    stage_a1(tb, tiles[i + 2], hts[i + 2], cs_[i + 2])
                if i + 1 < n_:
                    stage_a2(tb, tiles[i + 1], hts[i + 1], cs_[i + 1])
                stage_b_tail(tb, tiles[i], hts[i], cs_[i])

        def stage_a1(tb, ti, ht, c):
            t0 = ti * 256
            XT = tb["XT"][c % len(tb["XT"])]
            SQ = tb["SQ"][c % len(tb["SQ"])]
            bA = (0, 1, 6)[c % 3]
            P.dma("sp", XT.ap, DR["xT"][:, :, t0:t0 + 256], writes=XT.k())
            P.op("act", lambda e: e.activation(out=SQ.ap, in_=XT.ap, func=AF.Square), reads=XT.k(), writes=SQ.k())
            for kc in range(16):
                P.op("pe", lambda e, kc=kc: e.matmul(psum[:, bA, 0:256], lhsT=ONES.ap, rhs=SQ.ap[:, kc, :],
                                                     start=(kc == 0), stop=(kc == 15)),
                     reads=ONES.k() + SQ.r(kc * 256, kc * 256 + 256), writes=pk(bA, 0, 256))

        def stage_a2(tb, ti, ht, c):
            XT = tb["XT"][c % len(tb["XT"])]
            RS = tb["RS"][c % 2]
            bA = (0, 1, 6)[c % 3]
            rstd_from_ss(RS, psum[:, bA, 0:256], pk(bA, 0, 256), 1.0 / D)
            for kc in range(16):
                P.op("dve", lambda e, kc=kc: e.scalar_tensor_tensor(out=ht.ap[:, kc, :], in0=XT.ap[:, kc, :],
                                                                 scalar=GV.ap[:, kc:kc + 1], in1=RS.ap,
                                                                 op0=ALU.mult, op1=ALU.mult),
                     reads=XT.r(kc * 256, kc * 256 + 256) + GV.k() + RS.k(), writes=ht.r(kc * 256, kc * 256 + 256))

        def stage_b_mm(tb, ti, ht, c):
            bB = 2 + (c % 2)
            bC = 4 + (c % 2)
            WA = tb["WA"]
            for m in range(2):
                for kc in range(16):
                    P.op("pe", lambda e, kc=kc, m=m: e.matmul(psum[:, bB, m * 256:(m + 1) * 256],
                                                              lhsT=WA.ap[:, kc, m * 128:(m + 1) * 128], rhs=ht.ap[:, kc, :],
                                                              start=(kc == 0), stop=(kc == 15)),
                         reads=WA.k() + ht.r(kc * 256, kc * 256 + 256), writes=pk(bB, m * 256, m * 256 + 256))
            for m in range(2):
                for kc in range(16):
                    P.op("pe", lambda e, kc=kc, m=m: e.matmul(psum[0:64, bC, m * 256:(m + 1) * 256],
                                                              lhsT=WA.ap[:, kc, 256 + m * 64:256 + (m + 1) * 64], rhs=ht.ap[:, kc, :],
                                                              start=(kc == 0), stop=(kc == 15)),
                         reads=WA.k() + ht.r(kc * 256, kc * 256 + 256), writes=pk(bC, m * 256, m * 256 + 256))

        def stage_b_tail(tb, ti, ht, c):
            t0 = ti * 256
            bA = (0, 1, 6)[c % 3]
            bB = 2 + (c % 2)
            bC = 4 + (c % 2)
            CSB = tb["CS"][c % 2]
            R2 = tb["R2"][c % 2]
            SQ2 = tb["SQ2"][c % 2]
            T1 = tb["T1"][0]
            T2 = tb["T2"][0]
            P.dma("sp", CSB.ap[0:64], DR["cs"][:, :, t0:t0 + 256], writes=CSB.k())
            P.op("act", lambda e: e.activation(out=SQ2.ap, in_=psum[:, bB, :].rearrange("p (m t) -> p m t", m=2), func=AF.Square),
                 reads=pk(bB), writes=SQ2.k())
            for m in range(2):
                P.op("pe", lambda e, m=m: e.matmul(psum[:, bA, 256:512], lhsT=ONES.ap, rhs=SQ2.ap[:, m, :],
                                                   start=(m == 0), stop=(m == 1)),
                     reads=ONES.k() + SQ2.k(), writes=pk(bA, 256, 512))
            rstd_from_ss(R2, psum[:, bA, 256:512], pk(bA, 256, 512), 1.0 / 256)
            for m in range(2):
                P.op("dve", lambda e, m=m: e.scalar_tensor_tensor(out=CKV.ap[:, m, t0:t0 + 256], in0=psum[:, bB, m * 256:(m + 1) * 256],
                                                                 scalar=GV.ap[:, 20 + m:21 + m], in1=R2.ap,
                                                                 op0=ALU.mult, op1=ALU.mult),
                     reads=pk(bB, m * 256, m * 256 + 256) + GV.k() + R2.k(), writes=CKV.r(m * S + t0, m * S + t0 + 256))
            P.op("dve", lambda e: e.tensor_tensor(out=T1.ap[0:64], in0=psum[0:64, bC, 0:256], in1=CSB.ap[0:64, 0, :], op=ALU.mult),
                 reads=pk(bC, 0, 256) + CSB.k(), writes=T1.k())
            P.op("dve", lambda e: e.tensor_tensor(out=T2.ap[0:64], in0=psum[0:64, bC, 256:512], in1=CSB.ap[0:64, 1, :], op=ALU.mult),
                 reads=pk(bC, 256, 512) + CSB.k(), writes=T2.k())
            P.op("pool", lambda e: e.tensor_tensor(out=KRT.ap[0:64, t0:t0 + 256], in0=T1.ap[0:64], in1=T2.ap[0:64], op=ALU.add),
                 reads=T1.k() + T2.k(), writes=KRT.r(t0, t0 + 256))

        tb1 = alloc_tile_bufs(56, 3, 1, 0)
        assert tb1["end"] <= 156 * KIB, tb1["end"]
        HTN = [View(arena, (156 + 8 * j) * KIB, BF16, [16, 256]) for j in range(6)]
        if stop == 0:
            P.enabled = False
        P.dma("pool", tb1["WA"].ap, DR["wa"], writes=tb1["WA"].k())
        na_tiles = [31, 0, 1, 2, 3, 4]
        run_tiles(tb1, na_tiles, HTN)

        if stop == 1:
            P.enabled = False
        WB = [View(arena, (56 + 8 * i) * KIB, BF16, [16, 256]) for i in range(2)]
        NAQ = View(arena, 72 * KIB, BF16, [8, TOK])
        NAK = View(arena, 88 * KIB, BF16, [8, 1536])
        NAV = View(arena, 112 * KIB, BF16, [12, 8, 129])
        assert 112 * KIB + NAV.nbytes <= 140 * KIB
        P.op("pool", lambda e: e.memset(NAV.ap[:, :, :, 128:129], 1.0), writes=NAV.k())
        pbank = [0]
        for g in range(12):
            W = WB[g % 2]
            P.dma("pool", W.ap, DR["wna"][:, :, g * 256:(g + 1) * 256], writes=W.k())
            kind = g // 4
            h0 = (g % 4) * 2
            if kind in (0, 1):
                tiles = range(1, 5) if kind == 0 else range(6)
                for j in tiles:
                    for hh in range(2):
                        b = 6 + (pbank[0] % 2)
                        pbank[0] += 1
                        for kc in range(16):
                            P.op("pe", lambda e, kc=kc, hh=hh, j=j, b=b, W=W: e.matmul(
                                psum[:, b, 0:256], lhsT=W.ap[:, kc, hh * 128:(hh + 1) * 128], rhs=HTN[j].ap[:, kc, :],
                                start=(kc == 0), stop=(kc == 15)),
                                reads=W.k() + HTN[j].r(kc * 256, kc * 256 + 256), writes=pk(b, 0, 256))
                        if kind == 0:
                            dst = NAQ
                            lo = (h0 + hh) * TOK + (j - 1) * 256
                            oap = NAQ.ap[:, h0 + hh, (j - 1) * 256:j * 256]
                        else:
                            dst = NAK
                            lo = (h0 + hh) * 1536 + j * 256
                            oap = NAK.ap[:, h0 + hh, j * 256:(j + 1) * 256]
                        evac(oap, psum[:, b, 0:256], pk(b, 0, 256), dst.r(lo, lo + 256))
            else:
                for j in range(6):
                    for s in range(2):
                        b = 6 + (pbank[0] % 2)
                        pbank[0] += 1
                        for kc in range(16):
                            P.op("pe", lambda e, kc=kc, s=s, j=j, b=b, W=W: e.matmul(
                                psum[:, b, 0:256], lhsT=HTN[j].ap[:, kc, s * 128:(s + 1) * 128], rhs=W.ap[:, kc, :],
                                start=(kc == 0), stop=(kc == 15)),
                                reads=W.k() + HTN[j].r(kc * 256, kc * 256 + 256), writes=pk(b, 0, 256))
                        ck = j * 2 + s
                        lo = (ck * 8 + h0) * 129
                        evac(NAV.ap[:, ck, h0:h0 + 2, 0:128], psum[:, b, 0:256].rearrange("p (h d) -> p h d", h=2),
                             pk(b, 0, 256), NAV.r(lo, lo + 2 * 129))

        if stop == 3:
            P.enabled = False
        NBIAS = [View(arena, (140 + 15 * i) * KIB, F32, [5, 6, 128]) for i in range(2)]
        NOUT = View(arena, 170 * KIB, F32, [8, 1024])
        STB = [View(arena, (56 + 3 * i) * KIB, F32, [6, 128]) for i in range(2)]
        PTB = [View(arena, 62 * KIB + 1536 * i, BF16, [6, 128]) for i in range(2)]
        RCP = [View(arena, int(65 * KIB) + 512 * i, F32, [1]) for i in range(2)]
        slot_of_block = [0, 1, 2, 2, 2, 2, 3, 4]
        sb_of_block = [0, 0, 2, 4, 6, 8, 10, 12]
        na_scale = 128 ** -0.5
        def na_scores(it):
            h, blk = divmod(it, 8)
            bS = 0 + 2 * (it % 2)
            tok0 = sb_of_block[blk] * 64
            for j in range(6):
                bb = bS + (j // 4)
                cc = (j % 4) * 128
                P.op("pe", lambda e, j=j, bb=bb, cc=cc, h=h, blk=blk, tok0=tok0: e.matmul(
                    psum[:, bb, cc:cc + 128], lhsT=NAK.ap[:, h, tok0 + j * 128:tok0 + (j + 1) * 128],
                    rhs=NAQ.ap[:, h, blk * 128:(blk + 1) * 128], start=True, stop=True),
                    reads=NAK.r(h * 1536 + tok0 + j * 128, h * 1536 + tok0 + (j + 1) * 128)
                    + NAQ.r(h * TOK + blk * 128, h * TOK + (blk + 1) * 128),
                    writes=pk(bb, cc, cc + 128))

        def na_rest(it):
            h, blk = divmod(it, 8)
            NB = NBIAS[h % 2]
            STt = STB[it % 2]
            PTt = PTB[it % 2]
            RC = RCP[it % 2]
            bS = 0 + 2 * (it % 2)
            bO = 4 + (it % 2)
            slot = slot_of_block[blk]
            P.op("dve", lambda e: e.scalar_tensor_tensor(
                out=STt.ap[:, 0:4, :], in0=psum[:, bS, :].rearrange("p (c q) -> p c q", c=4), scalar=na_scale,
                in1=NB.ap[:, slot, 0:4, :], op0=ALU.mult, op1=ALU.add),
                reads=pk(bS) + NB.k(), writes=STt.r(0, 512))
            P.op("dve", lambda e: e.scalar_tensor_tensor(
                out=STt.ap[:, 4:6, :], in0=psum[:, bS + 1, 0:256].rearrange("p (c q) -> p c q", c=2), scalar=na_scale,
                in1=NB.ap[:, slot, 4:6, :], op0=ALU.mult, op1=ALU.add),
                reads=pk(bS + 1, 0, 256) + NB.k(), writes=STt.r(512, 768))
            P.op("act", lambda e: e.activation(out=PTt.ap, in_=STt.ap, func=AF.Exp), reads=STt.k(), writes=PTt.k())
            for j in range(6):
                ck = sb_of_block[blk] // 2 + j
                P.op("pe", lambda e, j=j, ck=ck: e.matmul(
                    psum[:, bO, 0:129], lhsT=PTt.ap[:, j, :], rhs=NAV.ap[:, ck, h, :], start=(j == 0), stop=(j == 5)),
                    reads=PTt.k() + NAV.r((ck * 8 + h) * 129, (ck * 8 + h + 1) * 129), writes=pk(bO, 0, 129))
            P.op("dve", lambda e: e.reciprocal(out=RC.ap, in_=psum[:, bO, 128:129]), reads=pk(bO, 0, 129), writes=RC.k())
            P.op("dve", lambda e: e.tensor_scalar_mul(
                out=NOUT.ap[:, blk, h * 128:(h + 1) * 128], in0=psum[:, bO, 0:128], scalar1=RC.ap),
                reads=pk(bO, 0, 129) + RC.k(), writes=NOUT.r(blk * 1024 + h * 128, blk * 1024 + (h + 1) * 128))

        na_scores(0)
        for it in range(64):
            h, blk = divmod(it, 8)
            if blk == 0:
                NB = NBIAS[h % 2]
                P.dma("sp", NB.ap, DR["nab"][h].rearrange("p (s c q) -> p s c q", s=5, c=6), writes=NB.k())
            if it + 1 < 64:
                na_scores(it + 1)
            na_rest(it)

        if debug:
            P.dma("sp", dbg["d_nout"], NOUT.ap.rearrange("p a b -> p (a b)"), reads=NOUT.k(), writes=["d_nout"], force=True)

        def norm_to_mixT(SRC, gcol, mix_base, bump_base_kib):
            b = Bump(arena, bump_base_kib)
            GB = b.take(F32, [1024])
            NN = [b.take(BF16, [1024]) for _ in range(2)]
            JUNK = b.take(BF16, [1024])
            SS = [b.take(F32, [1]) for _ in range(2)]
            P.dma("sp", GB.ap, DR["gbc"][:, gcol:gcol + 1024], writes=GB.k())
            def s1(tt):
                ss = SS[tt % 2]
                P.op("act", lambda e, tt=tt, ss=ss: e.activation(out=JUNK.ap, in_=SRC.ap[:, tt, :], func=AF.Square, accum_out=ss.ap),
                     reads=SRC.r(tt * 1024, tt * 1024 + 1024), writes=JUNK.k() + ss.k())
                rstd_from_ss(ss, ss.ap, ss.k(), 1.0 / 1024)

            s1(0)
            for tt in range(8):
                ss = SS[tt % 2]
                nn = NN[tt % 2]
                P.op("dve", lambda e, tt=tt, ss=ss, nn=nn: e.scalar_tensor_tensor(
                    out=nn.ap, in0=SRC.ap[:, tt, :], scalar=ss.ap, in1=GB.ap, op0=ALU.mult, op1=ALU.mult),
                    reads=SRC.r(tt * 1024, tt * 1024 + 1024) + ss.k() + GB.k(), writes=nn.k())
                bT = 6 + (tt % 2)
                pT = psum[:, bT, :].bitcast(BF16)
                for c8 in range(8):
                    P.op("pe", lambda e, c8=c8, nn=nn, pT=pT: e.transpose(pT[:, c8 * 128:(c8 + 1) * 128], nn.ap[:, c8 * 128:(c8 + 1) * 128], IDENT.ap),
                         reads=nn.k() + IDENT.k(), writes=pk(bT))
                if tt + 1 < 8:
                    s1(tt + 1)
                evac(MIXT.ap[:, mix_base:mix_base + 8, tt * 128:(tt + 1) * 128], pT.rearrange("p (c t) -> p c t", c=8),
                     pk(bT), sum([MIXT.r((mix_base + c8) * TOK + tt * 128, (mix_base + c8) * TOK + (tt + 1) * 128) for c8 in range(8)], []))

        norm_to_mixT(NOUT, 1024, 8, 88)

        if stop == 4:
            P.enabled = False
        tbq = alloc_tile_bufs(88, 1, 1, 1, kv=False)
        tbq["XT"].append(View(arena, 56 * KIB, F32, [16, 256]))
        bq = Bump(arena, 0)
        bq.o = tbq["end"]
        WCQ = bq.take(BF16, [16, 512])
        CQN = [bq.take(BF16, [4, 256]) for _ in range(2)]
        SQ3 = [bq.take(BF16, [4, 256]) for _ in range(1)]
        RQ = [bq.take(F32, [256]) for _ in range(1)]
        WUQ = bq.take(BF16, [4, 1536])
        WUQS = bq.take(BF16, [4, 512])
        CSQ = [bq.take(F32, [2, 256]) for _ in range(2)]
        TQ1 = bq.take(F32, [256])
        TQ2 = bq.take(F32, [256])
        assert bq.o <= 172 * KIB, bq.o
        P.op("pool", lambda e: e.memset(QR.ap[64:128], 0.0), writes=QR.k())
        P.dma("pool", WCQ.ap, DR["wcq"], writes=WCQ.k())
        P.dma("pool", WUQ.ap, DR["wuq"], writes=WUQ.k())
        P.dma("pool", WUQS.ap, DR["wuqs"], writes=WUQS.k())
        for ti in range(4):
            if stop == 4.4 and ti == 1:
                P.enabled = False
            ht = tbq["HT"][0]
            proc_tile(tbq, ti, ht, False)
            t0 = ti * 256
            cqn = CQN[ti % 2]
            rq = RQ[0]
            csq = CSQ[ti % 2]
            sq3 = SQ3[0]
            P.dma("sp", csq.ap[0:64], DR["cs"][:, :, t0:t0 + 256], writes=csq.k())
            for m in range(4):
                bb = 2 + m // 2
                for kc in range(16):
                    P.op("pe", lambda e, kc=kc, m=m, bb=bb, ht=ht: e.matmul(
                        psum[:, bb, (m % 2) * 256:(m % 2 + 1) * 256], lhsT=WCQ.ap[:, kc, m * 128:(m + 1) * 128], rhs=ht.ap[:, kc, :],
                        start=(kc == 0), stop=(kc == 15)),
                        reads=WCQ.k() + ht.r(kc * 256, kc * 256 + 256), writes=pk(bb, (m % 2) * 256, (m % 2) * 256 + 256))
            for hb in range(2):
                P.op("act", lambda e, hb=hb: e.activation(out=sq3.ap[:, 2 * hb:2 * hb + 2, :],
                                                         in_=psum[:, 2 + hb, :].rearrange("p (m t) -> p m t", m=2), func=AF.Square),
                     reads=pk(2 + hb), writes=sq3.r(hb * 512, hb * 512 + 512))
            bA = 4 + (ti % 2)
            for m in range(4):
                P.op("pe", lambda e, m=m, bA=bA: e.matmul(psum[:, bA, 0:256], lhsT=ONES.ap, rhs=sq3.ap[:, m, :], start=(m == 0), stop=(m == 3)),
                     reads=ONES.k() + sq3.k(), writes=pk(bA, 0, 256))
            rstd_from_ss(rq, psum[:, bA, 0:256], pk(bA, 0, 256), 1.0 / 512)
            for m in range(4):
                bb = 2 + m // 2
                P.op("dve", lambda e, m=m, bb=bb, cqn=cqn, rq=rq: e.scalar_tensor_tensor(
                    out=cqn.ap[:, m, :], in0=psum[:, bb, (m % 2) * 256:(m % 2 + 1) * 256], scalar=GV.ap[:, 16 + m:17 + m], in1=rq.ap,
                    op0=ALU.mult, op1=ALU.mult),
                    reads=pk(bb, (m % 2) * 256, (m % 2) * 256 + 256) + GV.k() + rq.k(), writes=cqn.r(m * 256, m * 256 + 256))
            if stop == 4.1:
                P.enabled = False
            for h in range(8):
                if stop == 4.3 and h == 1:
                    P.enabled = False
                bN = h % 2
                bR = 6 + (h % 2)
                for kc in range(4):
                    P.op("pe", lambda e, kc=kc, h=h, cqn=cqn, bN=bN: e.matmul(
                        psum[:, bN, 0:256], lhsT=WUQ.ap[:, kc, h * 192:h * 192 + 128], rhs=cqn.ap[:, kc, :],
                        start=(kc == 0), stop=(kc == 3)),
                        reads=WUQ.k() + cqn.k(), writes=pk(bN, (h % 2) * 256, (h % 2) * 256 + 256))
                evac(QN.ap[:, h, t0:t0 + 256], psum[:, bN, 0:256], pk(bN),
                     QN.r(h * TOK + t0, h * TOK + t0 + 256))
                if stop == 4.2:
                    P.enabled = False
                for kc in range(4):
                    P.op("pe", lambda e, kc=kc, h=h, cqn=cqn, bR=bR: e.matmul(
                        psum[0:64, bR, 0:256], lhsT=WUQ.ap[:, kc, h * 192 + 128:h * 192 + 192], rhs=cqn.ap[:, kc, :],
                        start=(kc == 0), stop=(kc == 3)),
                        reads=WUQ.k() + cqn.k(), writes=pk(bR, 0, 256))
                for kc in range(4):
                    P.op("pe", lambda e, kc=kc, h=h, cqn=cqn, bR=bR: e.matmul(
                        psum[0:64, bR, 256:512], lhsT=WUQS.ap[:, kc, h * 64:(h + 1) * 64], rhs=cqn.ap[:, kc, :],
                        start=(kc == 0), stop=(kc == 3)),
                        reads=WUQS.k() + cqn.k(), writes=pk(bR, 256, 512))
                P.op("dve", lambda e, csq=csq, bR=bR: e.tensor_tensor(out=TQ1.ap[0:64], in0=psum[0:64, bR, 0:256], in1=csq.ap[0:64, 0, :], op=ALU.mult),
                     reads=pk(bR, 0, 256) + csq.k(), writes=TQ1.k())
                P.op("dve", lambda e, csq=csq, bR=bR: e.tensor_tensor(out=TQ2.ap[0:64], in0=psum[0:64, bR, 256:512], in1=csq.ap[0:64, 1, :], op=ALU.mult),
                     reads=pk(bR, 256, 512) + csq.k(), writes=TQ2.k())
                P.op("pool", lambda e, h=h, t0=t0: e.tensor_tensor(out=QR.ap[0:64, h, t0:t0 + 256], in0=TQ1.ap[0:64], in1=TQ2.ap[0:64], op=ALU.add),
                     reads=TQ1.k() + TQ2.k(), writes=QR.r(h * TOK + t0, h * TOK + t0 + 256))

        if stop == 4.5:
            P.enabled = False
        tb5 = alloc_tile_bufs(88, 2, 1, 2)
        tb5["XT"].append(View(arena, 56 * KIB, F32, [16, 256]))
        assert tb5["end"] <= 172 * KIB, tb5["end"]
        P.dma("pool", tb5["WA"].ap, DR["wa"], writes=tb5["WA"].k())
        run_tiles(tb5, list(range(5, 31)), [tb5["HT"][i % 2] for i in range(26)])

        if debug:
            P.dma("sp", dbg["d_ckv"], CKV.ap.rearrange("p a b -> p (a b)"), reads=CKV.k(), writes=["d_ckv"], force=True)
            P.dma("sp", dbg["d_krt"], KRT.ap[0:64], reads=KRT.k(), writes=["d_krt"], force=True)
            P.dma("sp", dbg["d_qn"], QN.ap.rearrange("p a b -> p (a b)"), reads=QN.k(), writes=["d_qn"], force=True)
            P.dma("sp", dbg["d_qr"], QR.ap[0:64].rearrange("p a b -> p (a b)"), reads=QR.k(), writes=["d_qr"], force=True)

        if stop == 5:
            P.enabled = False
        bm = Bump(arena, 88)
        KB = [bm.take(BF16, [2048]) for _ in range(2)]
        VB = [bm.take(BF16, [16, 129]) for _ in range(2)]
        WUKV = bm.take(BF16, [2, 2048])
        AOUT = bm.take(F32, [8, 1024])
        PT = [bm.take(BF16, [512]) for _ in range(4)]
        RCM = [bm.take(F32, [1]) for _ in range(2)]
        assert bm.o <= 172 * KIB, bm.o
        P.dma("pool", WUKV.ap, DR["wukv"], writes=WUKV.k())
        WO = [View(arena, (152 + 16 * i) * KIB, BF16, [16, 512]) for i in range(2)]
        P.dma("pool", WO[0].ap, DR["wout"][:, :, 0:512], writes=WO[0].k())
        for v in VB:
            P.op("pool", lambda e, v=v: e.memset(v.ap[:, :, 128:129], 1.0), writes=v.k())
        mla_scale = 192 ** -0.5

        def okey(qt):
            return pk(qt // 3, (qt % 3) * 160, (qt % 3) * 160 + 129)

        def oap(qt, lo, hi):
            return psum[:, qt // 3, (qt % 3) * 160 + lo:(qt % 3) * 160 + hi]

        def gen_kv(n):
            h, kb = divmod(n, 4)
            kbuf = KB[n % 2]
            vbuf = VB[n % 2]
            k0 = kb * 2048
            for t4 in range(4):
                b = (6, 7)[t4 % 2]
                for c in range(2):
                    P.op("pe", lambda e, c=c, t4=t4, b=b, h=h, k0=k0: e.matmul(
                        psum[:, b, :], lhsT=WUKV.ap[:, c, h * 256:h * 256 + 128], rhs=CKV.ap[:, c, k0 + t4 * 512:k0 + (t4 + 1) * 512],
                        start=(c == 0), stop=(c == 1)),
                        reads=WUKV.k() + CKV.r(c * S + k0 + t4 * 512, c * S + k0 + (t4 + 1) * 512), writes=pk(b))
                evac(kbuf.ap[:, t4 * 512:(t4 + 1) * 512], psum[:, b, :], pk(b), kbuf.r(t4 * 512, (t4 + 1) * 512), eng="dve")
            for j4 in range(4):
                b = (6, 7)[j4 % 2]
                for jj in range(4):
                    j = j4 * 4 + jj
                    for c in range(2):
                        P.op("pe", lambda e, c=c, j=j, jj=jj, h=h, k0=k0, b=b: e.matmul(
                            psum[:, b, jj * 128:(jj + 1) * 128], lhsT=CKV.ap[:, c, k0 + j * 128:k0 + (j + 1) * 128],
                            rhs=WUKV.ap[:, c, h * 256 + 128:h * 256 + 256], start=(c == 0), stop=(c == 1)),
                            reads=WUKV.k() + CKV.r(c * S + k0 + j * 128, c * S + k0 + (j + 1) * 128), writes=pk(b, jj * 128, (jj + 1) * 128))
                evac(vbuf.ap[:, j4 * 4:(j4 + 1) * 4, 0:128], psum[:, b, :].rearrange("p (j d) -> p j d", j=4),
                     pk(b), vbuf.r(j4 * 4 * 129, (j4 + 1) * 4 * 129), eng="dve")

        sc_ctr = [0]

        def attn_block(n):
            h, kb = divmod(n, 4)
            kbuf = KB[n % 2]
            vbuf = VB[n % 2]
            k0 = kb * 2048
            steps = [(qg, j) for qg in range(2) for j in range(16)]

            def emit_scores(qg, j, i):
                bS = 3 + (i % 3)
                P.op("pe", lambda e, j=j, qg=qg, bS=bS: e.matmul(
                    psum[:, bS, :], lhsT=kbuf.ap[:, j * 128:(j + 1) * 128], rhs=QN.ap[:, h, qg * 512:(qg + 1) * 512], start=True, stop=False),
                    reads=kbuf.r(j * 128, (j + 1) * 128) + QN.r(h * TOK + qg * 512, h * TOK + (qg + 1) * 512), writes=pk(bS))
                P.op("pe", lambda e, j=j, qg=qg, bS=bS: e.matmul(
                    psum[:, bS, :], lhsT=KRT.ap[:, k0 + j * 128:k0 + (j + 1) * 128], rhs=QR.ap[:, h, qg * 512:(qg + 1) * 512],
                    start=False, stop=True),
                    reads=KRT.r(k0 + j * 128, k0 + (j + 1) * 128) + QR.r(h * TOK + qg * 512, h * TOK + (qg + 1) * 512), writes=pk(bS))

            base = sc_ctr[0]
            emit_scores(steps[0][0], steps[0][1], base)
            emit_scores(steps[1][0], steps[1][1], base + 1)
            for si, (qg, j) in enumerate(steps):
                i = base + si
                bS = 3 + (i % 3)
                pt = PT[i % 4]
                if si + 2 < len(steps):
                    emit_scores(steps[si + 2][0], steps[si + 2][1], i + 2)
                P.op("act", lambda e, bS=bS, pt=pt: e.activation(out=pt.ap, in_=psum[:, bS, :], func=AF.Exp, scale=mla_scale),
                     reads=pk(bS), writes=pt.k())
                for qs in range(4):
                    qt = qg * 4 + qs
                    P.op("pe", lambda e, qs=qs, qt=qt, j=j, pt=pt: e.matmul(
                        oap(qt, 0, 129), lhsT=pt.ap[:, qs * 128:(qs + 1) * 128], rhs=vbuf.ap[:, j, :],
                        start=(kb == 0 and j == 0 and qt % 3 == 0), stop=(kb == 3 and j == 15), skip_group_check=True),
                        reads=pt.k() + vbuf.r(j * 129, (j + 1) * 129), writes=okey(qt))
            sc_ctr[0] += len(steps)
            if kb == 3:
                for qt in range(8):
                    rc = RCM[qt % 2]
                    P.op("dve", lambda e, qt=qt, rc=rc: e.reciprocal(out=rc.ap, in_=oap(qt, 128, 129)), reads=okey(qt), writes=rc.k())
                    P.op("dve", lambda e, qt=qt, rc=rc: e.tensor_scalar_mul(
                        out=AOUT.ap[:, qt, h * 128:(h + 1) * 128], in0=oap(qt, 0, 128), scalar1=rc.ap),
                        reads=okey(qt) + rc.k(), writes=AOUT.r(qt * 1024 + h * 128, qt * 1024 + (h + 1) * 128))

        gen_kv(0)
        for n in range(32):
            if n + 1 < 32:
                gen_kv(n + 1)
            attn_block(n)

        if debug:
            P.dma("sp", dbg["d_aout"], AOUT.ap.rearrange("p a b -> p (a b)"), reads=AOUT.k(), writes=["d_aout"], force=True)

        norm_to_mixT(AOUT, 0, 0, 188)

        if stop == 6:
            P.enabled = False
        X1 = View(arena, 88 * KIB, F32, [8, D])
        W1 = [View(arena, (8 + 16 * i) * KIB, BF16, [16, 512]) for i in range(2)]
        W2 = [View(arena, (40 + 16 * i) * KIB, BF16, [4, D]) for i in range(2)]
        P.dma("pool", WO[1].ap, DR["wout"][:, :, 512:1024], writes=WO[1].k())
        for tt in range(8):
            P.dma("sp", X1.ap[:, tt, :], DR["xown"][tt * 128:(tt + 1) * 128, :], writes=X1.r(tt * D, (tt + 1) * D))
        pb7 = [0]
        for cg in range(4):
            W = WO[cg % 2]
            if cg >= 2:
                P.dma("pool", W.ap, DR["wout"][:, :, cg * 512:(cg + 1) * 512], writes=W.k())
            if cg == 3:
                P.dma("pool", W1[0].ap, DR["wff1"][:, :, 0:512], writes=W1[0].k())
                P.dma("pool", W2[0].ap, DR["wff2"][:, 0:4, :], writes=W2[0].k())
                P.dma("pool", W1[1].ap, DR["wff1"][:, :, 512:1024], writes=W1[1].k())
            for tt in range(8):
                b = pb7[0] % 4
                pb7[0] += 1
                for kc in range(16):
                    P.op("pe", lambda e, kc=kc, tt=tt, b=b, W=W: e.matmul(
                        psum[:, b, :], lhsT=MIXT.ap[:, kc, tt * 128:(tt + 1) * 128], rhs=W.ap[:, kc, :], start=(kc == 0), stop=(kc == 15)),
                        reads=MIXT.r(kc * TOK + tt * 128, kc * TOK + (tt + 1) * 128) + W.k(), writes=pk(b))
                lo = tt * D + cg * 512
                P.op("dve", lambda e, tt=tt, cg=cg, b=b: e.tensor_tensor(
                    out=X1.ap[:, tt, cg * 512:(cg + 1) * 512], in0=psum[:, b, :], in1=X1.ap[:, tt, cg * 512:(cg + 1) * 512], op=ALU.add),
                    reads=pk(b) + X1.r(lo, lo + 512), writes=X1.r(lo, lo + 512))
        if debug:
            P.dma("sp", dbg["d_x1"], X1.ap.rearrange("p a b -> p (a b)"), reads=X1.k(), writes=["d_x1"], force=True)

        H2T = View(arena, 152 * KIB, BF16, [16, TOK])
        b7 = Bump(arena, 188)
        GM = b7.take(F32, [D])
        H2 = [b7.take(BF16, [D]) for _ in range(2)]
        assert b7.o <= 204 * KIB
        JK = View(arena, 184 * KIB, BF16, [D])
        P.dma("sp", GM.ap, DR["gbc"][:, 2048:4096], writes=GM.k())
        def s1_mlp(tt):
            ss = SS7[tt % 2]
            P.op("act", lambda e, tt=tt, ss=ss: e.activation(out=JK.ap, in_=X1.ap[:, tt, :], func=AF.Square, accum_out=ss.ap),
                 reads=X1.r(tt * D, (tt + 1) * D), writes=JK.k() + ss.k())
            rstd_from_ss(ss, ss.ap, ss.k(), 1.0 / D)

        s1_mlp(0)
        for tt in range(8):
            ss = SS7[tt % 2]
            h2 = H2[tt % 2]
            if tt + 1 < 8:
                s1_mlp(tt + 1)
            P.op("dve", lambda e, tt=tt, ss=ss, h2=h2: e.scalar_tensor_tensor(
                out=h2.ap, in0=X1.ap[:, tt, :], scalar=ss.ap, in1=GM.ap, op0=ALU.mult, op1=ALU.mult),
                reads=X1.r(tt * D, (tt + 1) * D) + ss.k() + GM.k(), writes=h2.k())
            for half in range(2):
                bT = 4 + ((2 * tt + half) % 4)
                pT = psum[:, bT, :].bitcast(BF16)
                for c8 in range(8):
                    cc = half * 8 + c8
                    P.op("pe", lambda e, c8=c8, cc=cc, h2=h2, pT=pT: e.transpose(pT[:, c8 * 128:(c8 + 1) * 128], h2.ap[:, cc * 128:(cc + 1) * 128], IDENT.ap),
                         reads=h2.k() + IDENT.k(), writes=pk(bT))
                evac(H2T.ap[:, half * 8:(half + 1) * 8, tt * 128:(tt + 1) * 128], pT.rearrange("p (c t) -> p c t", c=8), pk(bT),
                     sum([H2T.r((half * 8 + c8) * TOK + tt * 128, (half * 8 + c8) * TOK + (tt + 1) * 128) for c8 in range(8)], []))

        if stop == 7:
            P.enabled = False
        AT = [View(arena, (72 + 8 * i) * KIB, BF16, [4, TOK]) for i in range(2)]
        RT = [View(arena, (184 + 2 * i) * KIB, F32, [512]) for i in range(2)]
        rt_ctr = [0]
        pf = [0]
        for g in range(16):
            w1 = W1[g % 2]
            w2 = W2[g % 2]
            at = AT[g % 2]
            if g >= 2:
                P.dma("pool", w1.ap, DR["wff1"][:, :, g * 512:(g + 1) * 512], writes=w1.k())
            if g >= 1:
                P.dma("pool", w2.ap, DR["wff2"][:, g * 4:(g + 1) * 4, :], writes=w2.k())
            for jc in range(4):
                for th in range(2):
                    b = pf[0] % 4
                    pf[0] += 1
                    for kc in range(16):
                        P.op("pe", lambda e, kc=kc, jc=jc, th=th, b=b, w1=w1: e.matmul(
                            psum[:, b, :], lhsT=w1.ap[:, kc, jc * 128:(jc + 1) * 128], rhs=H2T.ap[:, kc, th * 512:(th + 1) * 512],
                            start=(kc == 0), stop=(kc == 15)),
                            reads=w1.k() + H2T.r(kc * TOK + th * 512, kc * TOK + (th + 1) * 512), writes=pk(b))
                    rt = RT[rt_ctr[0] % 2]
                    rt_ctr[0] += 1
                    P.op("act", lambda e, b=b, rt=rt: e.activation(out=rt.ap, in_=psum[:, b, :], func=AF.Relu), reads=pk(b), writes=rt.k())
                    P.op("pool", lambda e, rt=rt, at=at, jc=jc, th=th: e.tensor_tensor(
                        out=at.ap[:, jc, th * 512:(th + 1) * 512], in0=rt.ap, in1=rt.ap, op=ALU.mult),
                        reads=rt.k(), writes=at.r(jc * TOK + th * 512, jc * TOK + (th + 1) * 512))
            for tt in range(8):
                for cg in range(4):
                    b = 4 + (pf[0] % 4)
                    pf[0] += 1
                    for jc in range(4):
                        P.op("pe", lambda e, jc=jc, tt=tt, cg=cg, b=b, at=at, w2=w2: e.matmul(
                            psum[:, b, :], lhsT=at.ap[:, jc, tt * 128:(tt + 1) * 128], rhs=w2.ap[:, jc, cg * 512:(cg + 1) * 512],
                            start=(jc == 0), stop=(jc == 3)),
                            reads=at.r(jc * TOK + tt * 128, jc * TOK + (tt + 1) * 128) + w2.r(jc * D + cg * 512, jc * D + (cg + 1) * 512),
                            writes=pk(b))
                    lo = tt * D + cg * 512
                    P.op("dve", lambda e, tt=tt, cg=cg, b=b: e.tensor_tensor(
                        out=X1.ap[:, tt, cg * 512:(cg + 1) * 512], in0=psum[:, b, :], in1=X1.ap[:, tt, cg * 512:(cg + 1) * 512], op=ALU.add),
                        reads=pk(b) + X1.r(lo, lo + 512), writes=X1.r(lo, lo + 512))

        b9 = Bump(arena, 8)
        GF = b9.take(F32, [D])
        OT = [b9.take(F32, [D]) for _ in range(2)]
        JK9 = b9.take(BF16, [D])
        SS9 = [b9.take(F32, [1]) for _ in range(2)]
        assert b9.o <= 40 * KIB
        P.dma("sp", GF.ap, DR["gbc"][:, 4096:6144], writes=GF.k())
        outs = []
        def s1_fin(tt):
            ss = SS9[tt % 2]
            P.op("act", lambda e, tt=tt, ss=ss: e.activation(out=JK9.ap, in_=X1.ap[:, tt, :], func=AF.Square, accum_out=ss.ap),
                 reads=X1.r(tt * D, (tt + 1) * D), writes=JK9.k() + ss.k())
            rstd_from_ss(ss, ss.ap, ss.k(), 1.0 / D)

        s1_fin(0)
        for tt in range(8):
            ss = SS9[tt % 2]
            ot = OT[tt % 2]
            if tt + 1 < 8:
                s1_fin(tt + 1)
            P.op("dve", lambda e, tt=tt, ss=ss, ot=ot: e.scalar_tensor_tensor(
                out=ot.ap, in0=X1.ap[:, tt, :], scalar=ss.ap, in1=GF.ap, op0=ALU.mult, op1=ALU.mult),
                reads=X1.r(tt * D, (tt + 1) * D) + ss.k() + GF.k(), writes=ot.k())
            P.dma("sp", y[tt * 128:(tt + 1) * 128, :], ot.ap, reads=ot.k(), writes=[("y", tt)])
            outs.append(("y", tt))
        P.fence("sp", outs + (list(dbg.keys()) if debug else []))
        P.finish(st)
        build_program.stats = P.stats
    return nc


def _pk(w, kchunks):
    kp, c = w.shape
    return np.ascontiguousarray(w.reshape(kchunks, 128, c).transpose(1, 0, 2))


def _swap_halves(w, nheads, width):
    r = w.reshape(w.shape[0], nheads, 2, width // 2)
    return np.ascontiguousarray(r[:, :, ::-1, :]).reshape(w.shape[0], nheads * width)


def _na_bias_tables(rpb, core):
    H = 8
    ROWS, W, KR, KC = 128, 64, 8, 16
    out = np.full((H, 5, 768, 128), NEG, dtype=np.float32)
    blocks_of_slot = {0: 0, 1: 1, 2: 3, 3: 6, 4: 7}
    sb_of_block = [0, 0, 2, 4, 6, 8, 10, 12]
    col = np.arange(W)
    cstart = np.clip(col - KC // 2, 0, W - KC)
    for slot, blk in blocks_of_slot.items():
        for dq in range(2):
            r = core * 16 + 2 * blk + dq
            rs = int(np.clip(r - KR // 2, 0, ROWS - KR))
            for i in range(KR):
                rk = rs + i
                lr = rk - core * 16 + 4 - sb_of_block[blk]
                assert 0 <= lr < 12
                dr = rk - r + (KR - 1)
                for c in range(W):
                    q = dq * 64 + c
                    ck = cstart[c] + np.arange(KC)
                    dc = ck - c + (KC - 1)
                    out[:, slot, lr * 64 + ck, q] = rpb[:, dr, dc]
    out = out.reshape(H, 5, 6, 128, 128).transpose(0, 3, 1, 2, 4)
    return np.ascontiguousarray(out).reshape(H, 128, 5 * 6 * 128)


_CACHE = {}


def kernel(x, attn_norm_g, w_in, q_norm_g, w_uq, kv_norm_g, w_ukv, na_rpb, mla_out_norm_g, na_out_norm_g,
           w_out, mlp_norm_g, w_ff1, w_ff2, final_norm_g, _debug=False, _stop=None, _cores=None):
    f = np.float32
    x = np.asarray(x, f)[0]
    w_in = np.asarray(w_in, f)[0]
    w_uq = np.asarray(w_uq, f)[0]
    w_ukv = np.asarray(w_ukv, f)[0]
    w_out = np.asarray(w_out, f)[0]
    w_ff1 = np.asarray(w_ff1, f)[0]
    w_ff2 = np.asarray(w_ff2, f)[0]
    rpb = np.asarray(na_rpb, f)[0]
    g_attn = np.asarray(attn_norm_g, f)[0]
    g_q = np.asarray(q_norm_g, f)[0]
    g_kv = np.asarray(kv_norm_g, f)[0]
    g_mla = np.asarray(mla_out_norm_g, f)[0]
    g_na = np.asarray(na_out_norm_g, f)[0]
    g_mlp = np.asarray(mlp_norm_g, f)[0]
    g_fin = np.asarray(final_norm_g, f)

    half = 32
    inv = (10000.0 ** (-np.arange(half, dtype=f) / half)).astype(f)
    ang = (np.arange(S, dtype=f)[:, None] * inv[None, :]).astype(f)
    cosT = np.cos(ang).astype(f).T
    sinT = np.sin(ang).astype(f).T
    cs = np.stack([np.concatenate([cosT, cosT], 0), np.concatenate([-sinT, sinT], 0)], axis=1)

    xT = _pk(np.ascontiguousarray(x.T), 16)
    w_rope = w_in[:, 768:832]
    wa = _pk(np.concatenate([w_in[:, 512:832], _swap_halves(w_rope, 1, 64)], axis=1), 16)
    wcq = _pk(w_in[:, 0:512], 16)
    wna = _pk(w_in[:, 832:3904], 16)
    wuq = _pk(w_uq, 4)
    uq_rope = w_uq.reshape(512, 8, 192)[:, :, 128:192].reshape(512, 512)
    wuqs = _pk(_swap_halves(uq_rope, 8, 64), 4)
    wukv = _pk(w_ukv, 2)
    wout = _pk(w_out, 16)
    wff1 = _pk(w_ff1, 16)
    wff2 = _pk(w_ff2, 64)
    gvec = np.concatenate([g_attn.reshape(16, 128).T, g_q.reshape(4, 128).T, g_kv.reshape(2, 128).T], axis=1)
    gvec = np.ascontiguousarray(gvec, dtype=f)
    gbc = np.ascontiguousarray(np.broadcast_to(np.concatenate([g_mla, g_na, g_mlp, g_fin])[None, :], (128, 6144)), dtype=f)

    ident = np.eye(128, dtype=f)
    key = (bool(_debug), _stop)
    if key not in _CACHE:
        _CACHE[key] = build_program(debug=_debug, stop=_stop)
    nc = _CACHE[key]

    in_maps = []
    cores = list(range(NCORES)) if _cores is None else list(_cores)
    for c in cores:
        sh = c * TOK
        in_maps.append(dict(
            xT=np.ascontiguousarray(np.roll(xT, -sh, axis=2)),
            xown=np.ascontiguousarray(x[sh:sh + TOK]),
            cs=np.ascontiguousarray(np.roll(cs, -sh, axis=2)),
            wa=wa, wcq=wcq, wna=wna, wuq=wuq, wuqs=wuqs, wukv=wukv, wout=wout, wff1=wff1, wff2=wff2,
            gvec=gvec, gbc=gbc, nab=_na_bias_tables(rpb, c), ident=ident,
        ))
    res = run_bass_kernel_spmd(nc, in_maps, core_ids=list(range(len(cores))))
    if _debug:
        kernel.last = res
        if _cores is not None:
            return None
    out = np.concatenate([np.asarray(r["y"], dtype=f) for r in res.results], axis=0)
    return out.reshape(1, S, D)
```

```python
import contextlib
import numpy as np
import concourse.bass as bass
import concourse.mybir as mybir
from concourse.bass_utils import run_bass_kernel_spmd

F32 = mybir.dt.float32
BF16 = mybir.dt.bfloat16
ALU = mybir.AluOpType
AF = mybir.ActivationFunctionType

NCORES = 8
S = 8192
D = 2048
TOK = S // NCORES
EPS = 1e-6
KIB = 1024
ARENA = 204 * KIB
CH = 512
NEG = -30000.0


class Prog:
    ENGS = ("pe", "act", "dve", "pool", "sp")
    NDMA = {"sp": 16, "pool": 16, "act": 2}

    def __init__(self, nc):
        self.nc = nc
        self.ops = []
        self.enabled = True

    def op(self, eng, fn, reads=(), writes=(), dma=False, force=False):
        if not (self.enabled or force):
            return
        self.ops.append(dict(eng=eng, fn=fn, reads=tuple(reads), writes=tuple(writes), dma=dma))

    def dma(self, q, out, in_, reads=(), writes=(), force=False):
        self.op(q, lambda e: e.dma_start(out=out, in_=in_), reads, writes, dma=True, force=force)

    def fence(self, eng, reads):
        self.op(eng, None, reads, (), force=True)

    def finish(self, stack):
        nc = self.nc
        ops = self.ops
        n = len(ops)
        lw = {}
        rd = {}
        need = [None] * n
        milestone = [False] * n
        lastacc = {}
        for i, o in enumerate(ops):
            d = set()
            excl = set(k for k in o["reads"] + o["writes"] if isinstance(k, tuple) and k[0] == "ps")
            for k in excl:
                la = lastacc.setdefault(k, {})
                for e2, j in la.items():
                    if e2 != o["eng"]:
                        d.add(j)
                la[o["eng"]] = i
            for r in o["reads"]:
                if r in excl:
                    continue
                j = lw.get(r)
                if j is not None:
                    d.add(j)
            for w in o["writes"]:
                if w in excl:
                    continue
                j = lw.get(w)
                if j is not None:
                    d.add(j)
                for j in rd.get(w, ()):
                    if ops[j]["eng"] == o["eng"] and o["eng"] in ("act", "dve") and not ops[j]["dma"] and not o["dma"]:
                        continue
                    d.add(j)
            d.discard(i)
            best = {}
            for j in d:
                oj = ops[j]
                if oj["dma"]:
                    best[("dma", j)] = j
                else:
                    if oj["eng"] == "pe" and o["eng"] == "pe" and not o["dma"]:
                        continue
                    key = ("eng", oj["eng"])
                    if key not in best or best[key] < j:
                        best[key] = j
            need[i] = sorted(best.values())
            for j in need[i]:
                if not ops[j]["dma"]:
                    milestone[j] = True
            for r in o["reads"]:
                if r not in excl:
                    rd.setdefault(r, []).append(i)
            for w in o["writes"]:
                if w not in excl:
                    lw[w] = i
                    rd[w] = []
        sems = {}
        for e in ("pe", "act", "dve", "pool"):
            sems[e] = stack.enter_context(nc.semaphore("s_" + e))
        dsems = {}
        for q, k in self.NDMA.items():
            dsems[q] = [stack.enter_context(nc.semaphore("d_%s%d" % (q, t))) for t in range(k)]
        tok = [None] * n
        dprev = [None] * n
        cnt = {e: 0 for e in sems}
        dcount = {q: 0 for q in dsems}
        dval = {q: [0] * len(dsems[q]) for q in dsems}
        for i, o in enumerate(ops):
            if o["fn"] is None:
                continue
            if o["dma"]:
                q = o["eng"]
                t = dcount[q] % len(dsems[q])
                dcount[q] += 1
                dprev[i] = (dsems[q][t], dval[q][t])
                dval[q][t] += 16
                tok[i] = (dsems[q][t], dval[q][t])
            elif milestone[i]:
                cnt[o["eng"]] += 1
                tok[i] = (sems[o["eng"]], cnt[o["eng"]])
        self.stats = dict(n_ops=n, milestones=dict(cnt), dmas=dict(dcount))
        per_eng = {e: [] for e in self.ENGS}
        for i, o in enumerate(ops):
            per_eng[o["eng"]].append(i)

        def emit(engname, e):
            waited = {}
            for i in per_eng[engname]:
                o = ops[i]
                for j in need[i]:
                    s, v = tok[j]
                    key = id(s)
                    if waited.get(key, 0) >= v:
                        continue
                    waited[key] = v
                    e.wait_ge(s, v)
                if o["fn"] is None:
                    continue
                if o["dma"] and dprev[i][1] > 0:
                    s, v = dprev[i]
                    if waited.get(id(s), 0) < v:
                        waited[id(s)] = v
                        e.wait_ge(s, v)
                ins = o["fn"](e)
                if tok[i] is not None:
                    ins.then_inc(tok[i][0], 16 if o["dma"] else 1)

        block = stack.enter_context(nc.Block())

        @block.tensor
        def _(e):
            emit("pe", e)

        @block.scalar
        def _(e):
            emit("act", e)

        @block.vector
        def _(e):
            emit("dve", e)

        @block.gpsimd
        def _(e):
            emit("pool", e)

        @block.sync
        def _(e):
            emit("sp", e)


class View:
    def __init__(self, arena, off, dtype, shape):
        self.off = off
        self.es = 2 if dtype == BF16 else 4
        self.shape = tuple(shape)
        n = 1
        for s in shape:
            n *= s
        self.n = n
        self.nbytes = n * self.es
        assert off % 4 == 0 and off + self.nbytes <= ARENA, (off, self.nbytes)
        a = arena[:, off // 2:(off + self.nbytes) // 2]
        if dtype != BF16:
            a = a.bitcast(dtype)
        if len(shape) == 2:
            a = a.rearrange("p (a b) -> p a b", a=shape[0])
        elif len(shape) == 3:
            a = a.rearrange("p (a b c) -> p a b c", a=shape[0], b=shape[1])
        elif len(shape) == 4:
            a = a.rearrange("p (a b c d) -> p a b c d", a=shape[0], b=shape[1], c=shape[2])
        self.ap = a

    def r(self, lo, hi):
        b0 = (self.off + lo * self.es) // CH
        b1 = (self.off + hi * self.es - 1) // CH
        return list(range(b0, b1 + 1))

    def k(self):
        return self.r(0, self.n)


class Bump:
    def __init__(self, arena, base_kib):
        self.arena = arena
        self.o = int(base_kib * KIB)

    def take(self, dtype, shape):
        v = View(self.arena, self.o, dtype, shape)
        self.o += (v.nbytes + CH - 1) // CH * CH
        return v


def build_program(debug=False, stop=None):
    nc = bass.Bass("TRN2", target_bir_lowering=False)
    DR = {}

    def din(name, shape):
        DR[name] = nc.dram_tensor(name, list(shape), F32, kind="ExternalInput").ap()

    din("xT", [128, 16, S])
    din("xown", [TOK, D])
    din("cs", [64, 2, S])
    din("wa", [128, 16, 384])
    din("wcq", [128, 16, 512])
    din("wna", [128, 16, 3072])
    din("wuq", [128, 4, 1536])
    din("wuqs", [128, 4, 512])
    din("wukv", [128, 2, 2048])
    din("wout", [128, 16, 2048])
    din("wff1", [128, 16, 8192])
    din("wff2", [128, 64, 2048])
    din("gvec", [128, 22])
    din("gbc", [128, 6144])
    din("nab", [8, 128, 3840])
    din("ident", [128, 128])
    y = nc.dram_tensor("y", [TOK, D], F32, kind="ExternalOutput").ap()
    dbg = {}
    if debug:
        for name, shape in (("d_ckv", [128, 2 * S]), ("d_krt", [64, S]), ("d_nout", [128, 8 * 1024]),
                            ("d_aout", [128, 8 * 1024]), ("d_x1", [128, 8 * 2048]), ("d_qn", [128, 8 * 1024]),
                            ("d_qr", [64, 8 * 1024])):
            dbg[name] = nc.dram_tensor(name, shape, F32 if name in ("d_nout", "d_aout", "d_x1") else BF16,
                                       kind="ExternalOutput").ap()

    st = contextlib.ExitStack()
    with st:
        arena = st.enter_context(nc.sbuf_tensor("arena", [128, ARENA // 2], BF16))
        psum = st.enter_context(nc.psum_tensor("ps", [128, 8, 512], F32))
        P = Prog(nc)

        def pk(b, lo=0, hi=512):
            return [("ps", b)]

        bp = Bump(arena, 0)
        ONES = bp.take(BF16, [128])
        IDENT = bp.take(BF16, [128])
        GV = bp.take(F32, [22])
        EPSV = bp.take(F32, [1])
        SS7 = [bp.take(F32, [1]) for _ in range(2)]
        assert bp.o <= 8 * KIB
        CKV = View(arena, 8 * KIB, BF16, [2, S])
        KRT = View(arena, 40 * KIB, BF16, [S])
        MIXT = View(arena, 56 * KIB, BF16, [16, TOK])
        QN = View(arena, 172 * KIB, BF16, [8, TOK])
        QR = View(arena, 188 * KIB, BF16, [8, TOK])

        P.op("dve", lambda e: e.memset(ONES.ap, 1.0), writes=ONES.k())
        P.op("dve", lambda e: e.memset(EPSV.ap, EPS), writes=EPSV.k())
        P.op("pool", lambda e: e.memset(KRT.ap[64:128], 0.0), writes=KRT.k())
        P.dma("sp", GV.ap, DR["gvec"], writes=GV.k())
        P.dma("pool", IDENT.ap, DR["ident"], writes=IDENT.k())

        evac_rr = [0]

        def evac(out_ap, in_ap, reads, writes, eng=None):
            if eng is None:
                eng = ("act", "dve")[evac_rr[0] % 2]
                evac_rr[0] += 1
            if eng == "act":
                P.op("act", lambda e: e.activation(out=out_ap, in_=in_ap, func=AF.Copy), reads, writes)
            else:
                P.op("dve", lambda e: e.tensor_copy(out=out_ap, in_=in_ap), reads, writes)

        def rstd_from_ss(dst, src_ap, src_keys, inv_n):
            P.op("act", lambda e: e.activation(out=dst.ap, in_=src_ap, func=AF.Ln, scale=inv_n, bias=EPSV.ap[0:dst.ap.shape[0]]),
                 reads=list(src_keys) + EPSV.k(), writes=dst.k())
            P.op("act", lambda e: e.activation(out=dst.ap, in_=dst.ap, func=AF.Exp, scale=-0.5), reads=dst.k(), writes=dst.k())

        def alloc_tile_bufs(base_kib, nxt, nsq, nht, kv=True):
            b = Bump(arena, base_kib)
            tb = dict()
            tb["XT"] = [b.take(F32, [16, 256]) for _ in range(nxt)]
            tb["SQ"] = [b.take(BF16, [16, 256]) for _ in range(nsq)]
            tb["HT"] = [b.take(BF16, [16, 256]) for _ in range(nht)]
            tb["RS"] = [b.take(F32, [256]) for _ in range(2)]
            if kv:
                tb["WA"] = b.take(BF16, [16, 384])
                tb["CS"] = [b.take(F32, [2, 256]) for _ in range(2)]
                tb["R2"] = [b.take(F32, [256]) for _ in range(2)]
                tb["SQ2"] = [b.take(BF16, [2, 256]) for _ in range(2)]
                tb["T1"] = [b.take(F32, [256]) for _ in range(1)]
                tb["T2"] = [b.take(F32, [256]) for _ in range(1)]
            tb["end"] = b.o
            return tb

        tile_ctr = [0]

        def proc_tile(tb, ti, ht, do_kv):
            c = tile_ctr[0]
            tile_ctr[0] += 1
            stage_a1(tb, ti, ht, c)
            stage_a2(tb, ti, ht, c)
            if do_kv:
                stage_b_mm(tb, ti, ht, c)
                stage_b_tail(tb, ti, ht, c)

        def run_tiles(tb, tiles, hts):
            cs_ = []
            for _ in tiles:
                cs_.append(tile_ctr[0])
                tile_ctr[0] += 1
            n_ = len(tiles)
            stage_a1(tb, tiles[0], hts[0], cs_[0])
            if n_ > 1:
                stage_a1(tb, tiles[1], hts[1], cs_[1])
            stage_a2(tb, tiles[0], hts[0], cs_[0])
            for i in range(n_):
                stage_b_mm(tb, tiles[i], hts[i], cs_[i])
                if i + 2 < n_:
                    stage_a1(tb, tiles[i + 2], hts[i + 2], cs_[i + 2])
                if i + 1 < n_:
                    stage_a2(tb, tiles[i + 1], hts[i + 1], cs_[i + 1])
                stage_b_tail(tb, tiles[i], hts[i], cs_[i])

        def stage_a1(tb, ti, ht, c):
            t0 = ti * 256
            XT = tb["XT"][c % len(tb["XT"])]
            SQ = tb["SQ"][c % len(tb["SQ"])]
            bA = (0, 1, 6)[c % 3]
            P.dma("sp", XT.ap, DR["xT"][:, :, t0:t0 + 256], writes=XT.k())
            P.op("act", lambda e: e.activation(out=SQ.ap, in_=XT.ap, func=AF.Square), reads=XT.k(), writes=SQ.k())
            for kc in range(16):
                P.op("pe", lambda e, kc=kc: e.matmul(psum[:, bA, 0:256], lhsT=ONES.ap, rhs=SQ.ap[:, kc, :],
                                                     start=(kc == 0), stop=(kc == 15)),
                     reads=ONES.k() + SQ.r(kc * 256, kc * 256 + 256), writes=pk(bA, 0, 256))

        def stage_a2(tb, ti, ht, c):
            XT = tb["XT"][c % len(tb["XT"])]
            RS = tb["RS"][c % 2]
            bA = (0, 1, 6)[c % 3]
            rstd_from_ss(RS, psum[:, bA, 0:256], pk(bA, 0, 256), 1.0 / D)
            for kc in range(16):
                P.op("dve", lambda e, kc=kc: e.scalar_tensor_tensor(out=ht.ap[:, kc, :], in0=XT.ap[:, kc, :],
                                                                 scalar=GV.ap[:, kc:kc + 1], in1=RS.ap,
                                                                 op0=ALU.mult, op1=ALU.mult),
                     reads=XT.r(kc * 256, kc * 256 + 256) + GV.k() + RS.k(), writes=ht.r(kc * 256, kc * 256 + 256))

        def stage_b_mm(tb, ti, ht, c):
            bB = 2 + (c % 2)
            bC = 4 + (c % 2)
            WA = tb["WA"]
            for m in range(2):
                for kc in range(16):
                    P.op("pe", lambda e, kc=kc, m=m: e.matmul(psum[:, bB, m * 256:(m + 1) * 256],
                                                              lhsT=WA.ap[:, kc, m * 128:(m + 1) * 128], rhs=ht.ap[:, kc, :],
                                                              start=(kc == 0), stop=(kc == 15)),
                         reads=WA.k() + ht.r(kc * 256, kc * 256 + 256), writes=pk(bB, m * 256, m * 256 + 256))
            for m in range(2):
                for kc in range(16):
                    P.op("pe", lambda e, kc=kc, m=m: e.matmul(psum[0:64, bC, m * 256:(m + 1) * 256],
                                                              lhsT=WA.ap[:, kc, 256 + m * 64:256 + (m + 1) * 64], rhs=ht.ap[:, kc, :],
                                                              start=(kc == 0), stop=(kc == 15)),
                         reads=WA.k() + ht.r(kc * 256, kc * 256 + 256), writes=pk(bC, m * 256, m * 256 + 256))

        def stage_b_tail(tb, ti, ht, c):
            t0 = ti * 256
            bA = (0, 1, 6)[c % 3]
            bB = 2 + (c % 2)
            bC = 4 + (c % 2)
            CSB = tb["CS"][c % 2]
            R2 = tb["R2"][c % 2]
            SQ2 = tb["SQ2"][c % 2]
            T1 = tb["T1"][0]
            T2 = tb["T2"][0]
            P.dma("sp", CSB.ap[0:64], DR["cs"][:, :, t0:t0 + 256], writes=CSB.k())
            P.op("act", lambda e: e.activation(out=SQ2.ap, in_=psum[:, bB, :].rearrange("p (m t) -> p m t", m=2), func=AF.Square),
                 reads=pk(bB), writes=SQ2.k())
            for m in range(2):
                P.op("pe", lambda e, m=m: e.matmul(psum[:, bA, 256:512], lhsT=ONES.ap, rhs=SQ2.ap[:, m, :],
                                                   start=(m == 0), stop=(m == 1)),
                     reads=ONES.k() + SQ2.k(), writes=pk(bA, 256, 512))
            rstd_from_ss(R2, psum[:, bA, 256:512], pk(bA, 256, 512), 1.0 / 256)
            for m in range(2):
                P.op("dve", lambda e, m=m: e.scalar_tensor_tensor(out=CKV.ap[:, m, t0:t0 + 256], in0=psum[:, bB, m * 256:(m + 1) * 256],
                                                                 scalar=GV.ap[:, 20 + m:21 + m], in1=R2.ap,
                                                                 op0=ALU.mult, op1=ALU.mult),
                     reads=pk(bB, m * 256, m * 256 + 256) + GV.k() + R2.k(), writes=CKV.r(m * S + t0, m * S + t0 + 256))
            P.op("dve", lambda e: e.tensor_tensor(out=T1.ap[0:64], in0=psum[0:64, bC, 0:256], in1=CSB.ap[0:64, 0, :], op=ALU.mult),
                 reads=pk(bC, 0, 256) + CSB.k(), writes=T1.k())
            P.op("dve", lambda e: e.tensor_tensor(out=T2.ap[0:64], in0=psum[0:64, bC, 256:512], in1=CSB.ap[0:64, 1, :], op=ALU.mult),
                 reads=pk(bC, 256, 512) + CSB.k(), writes=T2.k())
            P.op("pool", lambda e: e.tensor_tensor(out=KRT.ap[0:64, t0:t0 + 256], in0=T1.ap[0:64], in1=T2.ap[0:64], op=ALU.add),
                 reads=T1.k() + T2.k(), writes=KRT.r(t0, t0 + 256))

        tb1 = alloc_tile_bufs(56, 3, 1, 0)
        assert tb1["end"] <= 156 * KIB, tb1["end"]
        HTN = [View(arena, (156 + 8 * j) * KIB, BF16, [16, 256]) for j in range(6)]
        if stop == 0:
            P.enabled = False
        P.dma("pool", tb1["WA"].ap, DR["wa"], writes=tb1["WA"].k())
        na_tiles = [31, 0, 1, 2, 3, 4]
        run_tiles(tb1, na_tiles, HTN)

        if stop == 1:
            P.enabled = False
        WB = [View(arena, (56 + 8 * i) * KIB, BF16, [16, 256]) for i in range(2)]
        NAQ = View(arena, 72 * KIB, BF16, [8, TOK])
        NAK = View(arena, 88 * KIB, BF16, [8, 1536])
        NAV = View(arena, 112 * KIB, BF16, [12, 8, 129])
        assert 112 * KIB + NAV.nbytes <= 140 * KIB
        P.op("pool", lambda e: e.memset(NAV.ap[:, :, :, 128:129], 1.0), writes=NAV.k())
        pbank = [0]
        for g in range(12):
            W = WB[g % 2]
            P.dma("pool", W.ap, DR["wna"][:, :, g * 256:(g + 1) * 256], writes=W.k())
            kind = g // 4
            h0 = (g % 4) * 2
            if kind in (0, 1):
                tiles = range(1, 5) if kind == 0 else range(6)
                for j in tiles:
                    for hh in range(2):
                        b = 6 + (pbank[0] % 2)
                        pbank[0] += 1
                        for kc in range(16):
                            P.op("pe", lambda e, kc=kc, hh=hh, j=j, b=b, W=W: e.matmul(
                                psum[:, b, 0:256], lhsT=W.ap[:, kc, hh * 128:(hh + 1) * 128], rhs=HTN[j].ap[:, kc, :],
                                start=(kc == 0), stop=(kc == 15)),
                                reads=W.k() + HTN[j].r(kc * 256, kc * 256 + 256), writes=pk(b, 0, 256))
                        if kind == 0:
                            dst = NAQ
                            lo = (h0 + hh) * TOK + (j - 1) * 256
                            oap = NAQ.ap[:, h0 + hh, (j - 1) * 256:j * 256]
                        else:
                            dst = NAK
                            lo = (h0 + hh) * 1536 + j * 256
                            oap = NAK.ap[:, h0 + hh, j * 256:(j + 1) * 256]
                        evac(oap, psum[:, b, 0:256], pk(b, 0, 256), dst.r(lo, lo + 256))
            else:
                for j in range(6):
                    for s in range(2):
                        b = 6 + (pbank[0] % 2)
                        pbank[0] += 1
                        for kc in range(16):
                            P.op("pe", lambda e, kc=kc, s=s, j=j, b=b, W=W: e.matmul(
                                psum[:, b, 0:256], lhsT=HTN[j].ap[:, kc, s * 128:(s + 1) * 128], rhs=W.ap[:, kc, :],
                                start=(kc == 0), stop=(kc == 15)),
                                reads=W.k() + HTN[j].r(kc * 256, kc * 256 + 256), writes=pk(b, 0, 256))
                        ck = j * 2 + s
                        lo = (ck * 8 + h0) * 129
                        evac(NAV.ap[:, ck, h0:h0 + 2, 0:128], psum[:, b, 0:256].rearrange("p (h d) -> p h d", h=2),
                             pk(b, 0, 256), NAV.r(lo, lo + 2 * 129))

        if stop == 3:
            P.enabled = False
        NBIAS = [View(arena, (140 + 15 * i) * KIB, F32, [5, 6, 128]) for i in range(2)]
        NOUT = View(arena, 170 * KIB, F32, [8, 1024])
        STB = [View(arena, (56 + 3 * i) * KIB, F32, [6, 128]) for i in range(2)]
        PTB = [View(arena, 62 * KIB + 1536 * i, BF16, [6, 128]) for i in range(2)]
        RCP = [View(arena, int(65 * KIB) + 512 * i, F32, [1]) for i in range(2)]
        slot_of_block = [0, 1, 2, 2, 2, 2, 3, 4]
        sb_of_block = [0, 0, 2, 4, 6, 8, 10, 12]
        na_scale = 128 ** -0.5
        def na_scores(it):
            h, blk = divmod(it, 8)
            bS = 0 + 2 * (it % 2)
            tok0 = sb_of_block[blk] * 64
            for j in range(6):
                bb = bS + (j // 4)
                cc = (j % 4) * 128
                P.op("pe", lambda e, j=j, bb=bb, cc=cc, h=h, blk=blk, tok0=tok0: e.matmul(
                    psum[:, bb, cc:cc + 128], lhsT=NAK.ap[:, h, tok0 + j * 128:tok0 + (j + 1) * 128],
                    rhs=NAQ.ap[:, h, blk * 128:(blk + 1) * 128], start=True, stop=True),
                    reads=NAK.r(h * 1536 + tok0 + j * 128, h * 1536 + tok0 + (j + 1) * 128)
                    + NAQ.r(h * TOK + blk * 128, h * TOK + (blk + 1) * 128),
                    writes=pk(bb, cc, cc + 128))

        def na_rest(it):
            h, blk = divmod(it, 8)
            NB = NBIAS[h % 2]
            STt = STB[it % 2]
            PTt = PTB[it % 2]
            RC = RCP[it % 2]
            bS = 0 + 2 * (it % 2)
            bO = 4 + (it % 2)
            slot = slot_of_block[blk]
            P.op("dve", lambda e: e.scalar_tensor_tensor(
                out=STt.ap[:, 0:4, :], in0=psum[:, bS, :].rearrange("p (c q) -> p c q", c=4), scalar=na_scale,
                in1=NB.ap[:, slot, 0:4, :], op0=ALU.mult, op1=ALU.add),
                reads=pk(bS) + NB.k(), writes=STt.r(0, 512))
            P.op("dve", lambda e: e.scalar_tensor_tensor(
                out=STt.ap[:, 4:6, :], in0=psum[:, bS + 1, 0:256].rearrange("p (c q) -> p c q", c=2), scalar=na_scale,
                in1=NB.ap[:, slot, 4:6, :], op0=ALU.mult, op1=ALU.add),
                reads=pk(bS + 1, 0, 256) + NB.k(), writes=STt.r(512, 768))
            P.op("act", lambda e: e.activation(out=PTt.ap, in_=STt.ap, func=AF.Exp), reads=STt.k(), writes=PTt.k())
            for j in range(6):
                ck = sb_of_block[blk] // 2 + j
                P.op("pe", lambda e, j=j, ck=ck: e.matmul(
                    psum[:, bO, 0:129], lhsT=PTt.ap[:, j, :], rhs=NAV.ap[:, ck, h, :], start=(j == 0), stop=(j == 5)),
                    reads=PTt.k() + NAV.r((ck * 8 + h) * 129, (ck * 8 + h + 1) * 129), writes=pk(bO, 0, 129))
            P.op("dve", lambda e: e.reciprocal(out=RC.ap, in_=psum[:, bO, 128:129]), reads=pk(bO, 0, 129), writes=RC.k())
            P.op("dve", lambda e: e.tensor_scalar_mul(
                out=NOUT.ap[:, blk, h * 128:(h + 1) * 128], in0=psum[:, bO, 0:128], scalar1=RC.ap),
                reads=pk(bO, 0, 129) + RC.k(), writes=NOUT.r(blk * 1024 + h * 128, blk * 1024 + (h + 1) * 128))

        na_scores(0)
        for it in range(64):
            h, blk = divmod(it, 8)
            if blk == 0:
                NB = NBIAS[h % 2]
                P.dma("sp", NB.ap, DR["nab"][h].rearrange("p (s c q) -> p s c q", s=5, c=6), writes=NB.k())
            if it + 1 < 64:
                na_scores(it + 1)
            na_rest(it)

        if debug:
            P.dma("sp", dbg["d_nout"], NOUT.ap.rearrange("p a b -> p (a b)"), reads=NOUT.k(), writes=["d_nout"], force=True)

        def norm_to_mixT(SRC, gcol, mix_base, bump_base_kib):
            b = Bump(arena, bump_base_kib)
            GB = b.take(F32, [1024])
            NN = [b.take(BF16, [1024]) for _ in range(2)]
            JUNK = b.take(BF16, [1024])
            SS = [b.take(F32, [1]) for _ in range(2)]
            P.dma("sp", GB.ap, DR["gbc"][:, gcol:gcol + 1024], writes=GB.k())
            def s1(tt):
                ss = SS[tt % 2]
                P.op("act", lambda e, tt=tt, ss=ss: e.activation(out=JUNK.ap, in_=SRC.ap[:, tt, :], func=AF.Square, accum_out=ss.ap),
                     reads=SRC.r(tt * 1024, tt * 1024 + 1024), writes=JUNK.k() + ss.k())
                rstd_from_ss(ss, ss.ap, ss.k(), 1.0 / 1024)

            s1(0)
            for tt in range(8):
                ss = SS[tt % 2]
                nn = NN[tt % 2]
                P.op("dve", lambda e, tt=tt, ss=ss, nn=nn: e.scalar_tensor_tensor(
                    out=nn.ap, in0=SRC.ap[:, tt, :], scalar=ss.ap, in1=GB.ap, op0=ALU.mult, op1=ALU.mult),
                    reads=SRC.r(tt * 1024, tt * 1024 + 1024) + ss.k() + GB.k(), writes=nn.k())
                bT = 6 + (tt % 2)
                pT = psum[:, bT, :].bitcast(BF16)
                for c8 in range(8):
                    P.op("pe", lambda e, c8=c8, nn=nn, pT=pT: e.transpose(pT[:, c8 * 128:(c8 + 1) * 128], nn.ap[:, c8 * 128:(c8 + 1) * 128], IDENT.ap),
                         reads=nn.k() + IDENT.k(), writes=pk(bT))
                if tt + 1 < 8:
                    s1(tt + 1)
                evac(MIXT.ap[:, mix_base:mix_base + 8, tt * 128:(tt + 1) * 128], pT.rearrange("p (c t) -> p c t", c=8),
                     pk(bT), sum([MIXT.r((mix_base + c8) * TOK + tt * 128, (mix_base + c8) * TOK + (tt + 1) * 128) for c8 in range(8)], []))

        norm_to_mixT(NOUT, 1024, 8, 88)

        if stop == 4:
            P.enabled = False
        tbq = alloc_tile_bufs(88, 1, 1, 1, kv=False)
        tbq["XT"].append(View(arena, 56 * KIB, F32, [16, 256]))
        bq = Bump(arena, 0)
        bq.o = tbq["end"]
        WCQ = bq.take(BF16, [16, 512])
        CQN = [bq.take(BF16, [4, 256]) for _ in range(2)]
        SQ3 = [bq.take(BF16, [4, 256]) for _ in range(1)]
        RQ = [bq.take(F32, [256]) for _ in range(1)]
        WUQ = bq.take(BF16, [4, 1536])
        WUQS = bq.take(BF16, [4, 512])
        CSQ = [bq.take(F32, [2, 256]) for _ in range(2)]
        TQ1 = bq.take(F32, [256])
        TQ2 = bq.take(F32, [256])
        assert bq.o <= 172 * KIB, bq.o
        P.op("pool", lambda e: e.memset(QR.ap[64:128], 0.0), writes=QR.k())
        P.dma("pool", WCQ.ap, DR["wcq"], writes=WCQ.k())
        P.dma("pool", WUQ.ap, DR["wuq"], writes=WUQ.k())
        P.dma("pool", WUQS.ap, DR["wuqs"], writes=WUQS.k())
        for ti in range(4):
            if stop == 4.4 and ti == 1:
                P.enabled = False
            ht = tbq["HT"][0]
            proc_tile(tbq, ti, ht, False)
            t0 = ti * 256
            cqn = CQN[ti % 2]
            rq = RQ[0]
            csq = CSQ[ti % 2]
            sq3 = SQ3[0]
            P.dma("sp", csq.ap[0:64], DR["cs"][:, :, t0:t0 + 256], writes=csq.k())
            for m in range(4):
                bb = 2 + m // 2
                for kc in range(16):
                    P.op("pe", lambda e, kc=kc, m=m, bb=bb, ht=ht: e.matmul(
                        psum[:, bb, (m % 2) * 256:(m % 2 + 1) * 256], lhsT=WCQ.ap[:, kc, m * 128:(m + 1) * 128], rhs=ht.ap[:, kc, :],
                        start=(kc == 0), stop=(kc == 15)),
                        reads=WCQ.k() + ht.r(kc * 256, kc * 256 + 256), writes=pk(bb, (m % 2) * 256, (m % 2) * 256 + 256))
            for hb in range(2):
                P.op("act", lambda e, hb=hb: e.activation(out=sq3.ap[:, 2 * hb:2 * hb + 2, :],
                                                         in_=psum[:, 2 + hb, :].rearrange("p (m t) -> p m t", m=2), func=AF.Square),
                     reads=pk(2 + hb), writes=sq3.r(hb * 512, hb * 512 + 512))
            bA = 4 + (ti % 2)
            for m in range(4):
                P.op("pe", lambda e, m=m, bA=bA: e.matmul(psum[:, bA, 0:256], lhsT=ONES.ap, rhs=sq3.ap[:, m, :], start=(m == 0), stop=(m == 3)),
                     reads=ONES.k() + sq3.k(), writes=pk(bA, 0, 256))
            rstd_from_ss(rq, psum[:, bA, 0:256], pk(bA, 0, 256), 1.0 / 512)
            for m in range(4):
                bb = 2 + m // 2
                P.op("dve", lambda e, m=m, bb=bb, cqn=cqn, rq=rq: e.scalar_tensor_tensor(
                    out=cqn.ap[:, m, :], in0=psum[:, bb, (m % 2) * 256:(m % 2 + 1) * 256], scalar=GV.ap[:, 16 + m:17 + m], in1=rq.ap,
                    op0=ALU.mult, op1=ALU.mult),
                    reads=pk(bb, (m % 2) * 256, (m % 2) * 256 + 256) + GV.k() + rq.k(), writes=cqn.r(m * 256, m * 256 + 256))
            if stop == 4.1:
                P.enabled = False
            for h in range(8):
                if stop == 4.3 and h == 1:
                    P.enabled = False
                bN = h % 2
                bR = 6 + (h % 2)
                for kc in range(4):
                    P.op("pe", lambda e, kc=kc, h=h, cqn=cqn, bN=bN: e.matmul(
                        psum[:, bN, 0:256], lhsT=WUQ.ap[:, kc, h * 192:h * 192 + 128], rhs=cqn.ap[:, kc, :],
                        start=(kc == 0), stop=(kc == 3)),
                        reads=WUQ.k() + cqn.k(), writes=pk(bN, (h % 2) * 256, (h % 2) * 256 + 256))
                evac(QN.ap[:, h, t0:t0 + 256], psum[:, bN, 0:256], pk(bN),
                     QN.r(h * TOK + t0, h * TOK + t0 + 256))
                if stop == 4.2:
                    P.enabled = False
                for kc in range(4):
                    P.op("pe", lambda e, kc=kc, h=h, cqn=cqn, bR=bR: e.matmul(
                        psum[0:64, bR, 0:256], lhsT=WUQ.ap[:, kc, h * 192 + 128:h * 192 + 192], rhs=cqn.ap[:, kc, :],
                        start=(kc == 0), stop=(kc == 3)),
                        reads=WUQ.k() + cqn.k(), writes=pk(bR, 0, 256))
                for kc in range(4):
                    P.op("pe", lambda e, kc=kc, h=h, cqn=cqn, bR=bR: e.matmul(
                        psum[0:64, bR, 256:512], lhsT=WUQS.ap[:, kc, h * 64:(h + 1) * 64], rhs=cqn.ap[:, kc, :],
                        start=(kc == 0), stop=(kc == 3)),
                        reads=WUQS.k() + cqn.k(), writes=pk(bR, 256, 512))
                P.op("dve", lambda e, csq=csq, bR=bR: e.tensor_tensor(out=TQ1.ap[0:64], in0=psum[0:64, bR, 0:256], in1=csq.ap[0:64, 0, :], op=ALU.mult),
                     reads=pk(bR, 0, 256) + csq.k(), writes=TQ1.k())
                P.op("dve", lambda e, csq=csq, bR=bR: e.tensor_tensor(out=TQ2.ap[0:64], in0=psum[0:64, bR, 256:512], in1=csq.ap[0:64, 1, :], op=ALU.mult),
                     reads=pk(bR, 256, 512) + csq.k(), writes=TQ2.k())
                P.op("pool", lambda e, h=h, t0=t0: e.tensor_tensor(out=QR.ap[0:64, h, t0:t0 + 256], in0=TQ1.ap[0:64], in1=TQ2.ap[0:64], op=ALU.add),
                     reads=TQ1.k() + TQ2.k(), writes=QR.r(h * TOK + t0, h * TOK + t0 + 256))

        if stop == 4.5:
            P.enabled = False
        tb5 = alloc_tile_bufs(88, 2, 1, 2)
        tb5["XT"].append(View(arena, 56 * KIB, F32, [16, 256]))
        assert tb5["end"] <= 172 * KIB, tb5["end"]
        P.dma("pool", tb5["WA"].ap, DR["wa"], writes=tb5["WA"].k())
        run_tiles(tb5, list(range(5, 31)), [tb5["HT"][i % 2] for i in range(26)])

        if debug:
            P.dma("sp", dbg["d_ckv"], CKV.ap.rearrange("p a b -> p (a b)"), reads=CKV.k(), writes=["d_ckv"], force=True)
            P.dma("sp", dbg["d_krt"], KRT.ap[0:64], reads=KRT.k(), writes=["d_krt"], force=True)
            P.dma("sp", dbg["d_qn"], QN.ap.rearrange("p a b -> p (a b)"), reads=QN.k(), writes=["d_qn"], force=True)
            P.dma("sp", dbg["d_qr"], QR.ap[0:64].rearrange("p a b -> p (a b)"), reads=QR.k(), writes=["d_qr"], force=True)

        if stop == 5:
            P.enabled = False
        bm = Bump(arena, 88)
        KB = [bm.take(BF16, [2048]) for _ in range(2)]
        VB = [bm.take(BF16, [16, 129]) for _ in range(2)]
        WUKV = bm.take(BF16, [2, 2048])
        AOUT = bm.take(F32, [8, 1024])
        PT = [bm.take(BF16, [512]) for _ in range(4)]
        RCM = [bm.take(F32, [1]) for _ in range(2)]
        assert bm.o <= 172 * KIB, bm.o
        P.dma("pool", WUKV.ap, DR["wukv"], writes=WUKV.k())
        WO = [View(arena, (152 + 16 * i) * KIB, BF16, [16, 512]) for i in range(2)]
        P.dma("pool", WO[0].ap, DR["wout"][:, :, 0:512], writes=WO[0].k())
        for v in VB:
            P.op("pool", lambda e, v=v: e.memset(v.ap[:, :, 128:129], 1.0), writes=v.k())
        mla_scale = 192 ** -0.5

        def okey(qt):
            return pk(qt // 3, (qt % 3) * 160, (qt % 3) * 160 + 129)

        def oap(qt, lo, hi):
            return psum[:, qt // 3, (qt % 3) * 160 + lo:(qt % 3) * 160 + hi]

        def gen_kv(n):
            h, kb = divmod(n, 4)
            kbuf = KB[n % 2]
            vbuf = VB[n % 2]
            k0 = kb * 2048
            for t4 in range(4):
                b = (6, 7)[t4 % 2]
                for c in range(2):
                    P.op("pe", lambda e, c=c, t4=t4, b=b, h=h, k0=k0: e.matmul(
                        psum[:, b, :], lhsT=WUKV.ap[:, c, h * 256:h * 256 + 128], rhs=CKV.ap[:, c, k0 + t4 * 512:k0 + (t4 + 1) * 512],
                        start=(c == 0), stop=(c == 1)),
                        reads=WUKV.k() + CKV.r(c * S + k0 + t4 * 512, c * S + k0 + (t4 + 1) * 512), writes=pk(b))
                evac(kbuf.ap[:, t4 * 512:(t4 + 1) * 512], psum[:, b, :], pk(b), kbuf.r(t4 * 512, (t4 + 1) * 512), eng="dve")
            for j4 in range(4):
                b = (6, 7)[j4 % 2]
                for jj in range(4):
                    j = j4 * 4 + jj
                    for c in range(2):
                        P.op("pe", lambda e, c=c, j=j, jj=jj, h=h, k0=k0, b=b: e.matmul(
                            psum[:, b, jj * 128:(jj + 1) * 128], lhsT=CKV.ap[:, c, k0 + j * 128:k0 + (j + 1) * 128],
                            rhs=WUKV.ap[:, c, h * 256 + 128:h * 256 + 256], start=(c == 0), stop=(c == 1)),
                            reads=WUKV.k() + CKV.r(c * S + k0 + j * 128, c * S + k0 + (j + 1) * 128), writes=pk(b, jj * 128, (jj + 1) * 128))
                evac(vbuf.ap[:, j4 * 4:(j4 + 1) * 4, 0:128], psum[:, b, :].rearrange("p (j d) -> p j d", j=4),
                     pk(b), vbuf.r(j4 * 4 * 129, (j4 + 1) * 4 * 129), eng="dve")

        sc_ctr = [0]

        def attn_block(n):
            h, kb = divmod(n, 4)
            kbuf = KB[n % 2]
            vbuf = VB[n % 2]
            k0 = kb * 2048
            steps = [(qg, j) for qg in range(2) for j in range(16)]

            def emit_scores(qg, j, i):
                bS = 3 + (i % 3)
                P.op("pe", lambda e, j=j, qg=qg, bS=bS: e.matmul(
                    psum[:, bS, :], lhsT=kbuf.ap[:, j * 128:(j + 1) * 128], rhs=QN.ap[:, h, qg * 512:(qg + 1) * 512], start=True, stop=False),
                    reads=kbuf.r(j * 128, (j + 1) * 128) + QN.r(h * TOK + qg * 512, h * TOK + (qg + 1) * 512), writes=pk(bS))
                P.op("pe", lambda e, j=j, qg=qg, bS=bS: e.matmul(
                    psum[:, bS, :], lhsT=KRT.ap[:, k0 + j * 128:k0 + (j + 1) * 128], rhs=QR.ap[:, h, qg * 512:(qg + 1) * 512],
                    start=False, stop=True),
                    reads=KRT.r(k0 + j * 128, k0 + (j + 1) * 128) + QR.r(h * TOK + qg * 512, h * TOK + (qg + 1) * 512), writes=pk(bS))

            base = sc_ctr[0]
            emit_scores(steps[0][0], steps[0][1], base)
            emit_scores(steps[1][0], steps[1][1], base + 1)
            for si, (qg, j) in enumerate(steps):
                i = base + si
                bS = 3 + (i % 3)
                pt = PT[i % 4]
                if si + 2 < len(steps):
                    emit_scores(steps[si + 2][0], steps[si + 2][1], i + 2)
                P.op("act", lambda e, bS=bS, pt=pt: e.activation(out=pt.ap, in_=psum[:, bS, :], func=AF.Exp, scale=mla_scale),
                     reads=pk(bS), writes=pt.k())
                for qs in range(4):
                    qt = qg * 4 + qs
                    P.op("pe", lambda e, qs=qs, qt=qt, j=j, pt=pt: e.matmul(
                        oap(qt, 0, 129), lhsT=pt.ap[:, qs * 128:(qs + 1) * 128], rhs=vbuf.ap[:, j, :],
                        start=(kb == 0 and j == 0 and qt % 3 == 0), stop=(kb == 3 and j == 15), skip_group_check=True),
                        reads=pt.k() + vbuf.r(j * 129, (j + 1) * 129), writes=okey(qt))
            sc_ctr[0] += len(steps)
            if kb == 3:
                for qt in range(8):
                    rc = RCM[qt % 2]
                    P.op("dve", lambda e, qt=qt, rc=rc: e.reciprocal(out=rc.ap, in_=oap(qt, 128, 129)), reads=okey(qt), writes=rc.k())
                    P.op("dve", lambda e, qt=qt, rc=rc: e.tensor_scalar_mul(
                        out=AOUT.ap[:, qt, h * 128:(h + 1) * 128], in0=oap(qt, 0, 128), scalar1=rc.ap),
                        reads=okey(qt) + rc.k(), writes=AOUT.r(qt * 1024 + h * 128, qt * 1024 + (h + 1) * 128))

        gen_kv(0)
        for n in range(32):
            if n + 1 < 32:
                gen_kv(n + 1)
            attn_block(n)

        if debug:
            P.dma("sp", dbg["d_aout"], AOUT.ap.rearrange("p a b -> p (a b)"), reads=AOUT.k(), writes=["d_aout"], force=True)

        norm_to_mixT(AOUT, 0, 0, 188)

        if stop == 6:
            P.enabled = False
        X1 = View(arena, 88 * KIB, F32, [8, D])
        W1 = [View(arena, (8 + 16 * i) * KIB, BF16, [16, 512]) for i in range(2)]
        W2 = [View(arena, (40 + 16 * i) * KIB, BF16, [4, D]) for i in range(2)]
        P.dma("pool", WO[1].ap, DR["wout"][:, :, 512:1024], writes=WO[1].k())
        for tt in range(8):
            P.dma("sp", X1.ap[:, tt, :], DR["xown"][tt * 128:(tt + 1) * 128, :], writes=X1.r(tt * D, (tt + 1) * D))
        pb7 = [0]
        for cg in range(4):
            W = WO[cg % 2]
            if cg >= 2:
                P.dma("pool", W.ap, DR["wout"][:, :, cg * 512:(cg + 1) * 512], writes=W.k())
            if cg == 3:
                P.dma("pool", W1[0].ap, DR["wff1"][:, :, 0:512], writes=W1[0].k())
                P.dma("pool", W2[0].ap, DR["wff2"][:, 0:4, :], writes=W2[0].k())
                P.dma("pool", W1[1].ap, DR["wff1"][:, :, 512:1024], writes=W1[1].k())
            for tt in range(8):
                b = pb7[0] % 4
                pb7[0] += 1
                for kc in range(16):
                    P.op("pe", lambda e, kc=kc, tt=tt, b=b, W=W: e.matmul(
                        psum[:, b, :], lhsT=MIXT.ap[:, kc, tt * 128:(tt + 1) * 128], rhs=W.ap[:, kc, :], start=(kc == 0), stop=(kc == 15)),
                        reads=MIXT.r(kc * TOK + tt * 128, kc * TOK + (tt + 1) * 128) + W.k(), writes=pk(b))
                lo = tt * D + cg * 512
                P.op("dve", lambda e, tt=tt, cg=cg, b=b: e.tensor_tensor(
                    out=X1.ap[:, tt, cg * 512:(cg + 1) * 512], in0=psum[:, b, :], in1=X1.ap[:, tt, cg * 512:(cg + 1) * 512], op=ALU.add),
                    reads=pk(b) + X1.r(lo, lo + 512), writes=X1.r(lo, lo + 512))
        if debug:
            P.dma("sp", dbg["d_x1"], X1.ap.rearrange("p a b -> p (a b)"), reads=X1.k(), writes=["d_x1"], force=True)

        H2T = View(arena, 152 * KIB, BF16, [16, TOK])
        b7 = Bump(arena, 188)
        GM = b7.take(F32, [D])
        H2 = [b7.take(BF16, [D]) for _ in range(2)]
        assert b7.o <= 204 * KIB
        JK = View(arena, 184 * KIB, BF16, [D])
        P.dma("sp", GM.ap, DR["gbc"][:, 2048:4096], writes=GM.k())
        def s1_mlp(tt):
            ss = SS7[tt % 2]
            P.op("act", lambda e, tt=tt, ss=ss: e.activation(out=JK.ap, in_=X1.ap[:, tt, :], func=AF.Square, accum_out=ss.ap),
                 reads=X1.r(tt * D, (tt + 1) * D), writes=JK.k() + ss.k())
            rstd_from_ss(ss, ss.ap, ss.k(), 1.0 / D)

        s1_mlp(0)
        for tt in range(8):
            ss = SS7[tt % 2]
            h2 = H2[tt % 2]
            if tt + 1 < 8:
                s1_mlp(tt + 1)
            P.op("dve", lambda e, tt=tt, ss=ss, h2=h2: e.scalar_tensor_tensor(
                out=h2.ap, in0=X1.ap[:, tt, :], scalar=ss.ap, in1=GM.ap, op0=ALU.mult, op1=ALU.mult),
                reads=X1.r(tt * D, (tt + 1) * D) + ss.k() + GM.k(), writes=h2.k())
            for half in range(2):
                bT = 4 + ((2 * tt + half) % 4)
                pT = psum[:, bT, :].bitcast(BF16)
                for c8 in range(8):
                    cc = half * 8 + c8
                    P.op("pe", lambda e, c8=c8, cc=cc, h2=h2, pT=pT: e.transpose(pT[:, c8 * 128:(c8 + 1) * 128], h2.ap[:, cc * 128:(cc + 1) * 128], IDENT.ap),
                         reads=h2.k() + IDENT.k(), writes=pk(bT))
                evac(H2T.ap[:, half * 8:(half + 1) * 8, tt * 128:(tt + 1) * 128], pT.rearrange("p (c t) -> p c t", c=8), pk(bT),
                     sum([H2T.r((half * 8 + c8) * TOK + tt * 128, (half * 8 + c8) * TOK + (tt + 1) * 128) for c8 in range(8)], []))

        if stop == 7:
            P.enabled = False
        AT = [View(arena, (72 + 8 * i) * KIB, BF16, [4, TOK]) for i in range(2)]
        RT = [View(arena, (184 + 2 * i) * KIB, F32, [512]) for i in range(2)]
        rt_ctr = [0]
        pf = [0]
        for g in range(16):
            w1 = W1[g % 2]
            w2 = W2[g % 2]
            at = AT[g % 2]
            if g >= 2:
                P.dma("pool", w1.ap, DR["wff1"][:, :, g * 512:(g + 1) * 512], writes=w1.k())
            if g >= 1:
                P.dma("pool", w2.ap, DR["wff2"][:, g * 4:(g + 1) * 4, :], writes=w2.k())
            for jc in range(4):
                for th in range(2):
                    b = pf[0] % 4
                    pf[0] += 1
                    for kc in range(16):
                        P.op("pe", lambda e, kc=kc, jc=jc, th=th, b=b, w1=w1: e.matmul(
                            psum[:, b, :], lhsT=w1.ap[:, kc, jc * 128:(jc + 1) * 128], rhs=H2T.ap[:, kc, th * 512:(th + 1) * 512],
                            start=(kc == 0), stop=(kc == 15)),
                            reads=w1.k() + H2T.r(kc * TOK + th * 512, kc * TOK + (th + 1) * 512), writes=pk(b))
                    rt = RT[rt_ctr[0] % 2]
                    rt_ctr[0] += 1
                    P.op("act", lambda e, b=b, rt=rt: e.activation(out=rt.ap, in_=psum[:, b, :], func=AF.Relu), reads=pk(b), writes=rt.k())
                    P.op("pool", lambda e, rt=rt, at=at, jc=jc, th=th: e.tensor_tensor(
                        out=at.ap[:, jc, th * 512:(th + 1) * 512], in0=rt.ap, in1=rt.ap, op=ALU.mult),
                        reads=rt.k(), writes=at.r(jc * TOK + th * 512, jc * TOK + (th + 1) * 512))
            for tt in range(8):
                for cg in range(4):
                    b = 4 + (pf[0] % 4)
                    pf[0] += 1
                    for jc in range(4):
                        P.op("pe", lambda e, jc=jc, tt=tt, cg=cg, b=b, at=at, w2=w2: e.matmul(
                            psum[:, b, :], lhsT=at.ap[:, jc, tt * 128:(tt + 1) * 128], rhs=w2.ap[:, jc, cg * 512:(cg + 1) * 512],
                            start=(jc == 0), stop=(jc == 3)),
                            reads=at.r(jc * TOK + tt * 128, jc * TOK + (tt + 1) * 128) + w2.r(jc * D + cg * 512, jc * D + (cg + 1) * 512),
                            writes=pk(b))
                    lo = tt * D + cg * 512
                    P.op("dve", lambda e, tt=tt, cg=cg, b=b: e.tensor_tensor(
                        out=X1.ap[:, tt, cg * 512:(cg + 1) * 512], in0=psum[:, b, :], in1=X1.ap[:, tt, cg * 512:(cg + 1) * 512], op=ALU.add),
                        reads=pk(b) + X1.r(lo, lo + 512), writes=X1.r(lo, lo + 512))

        b9 = Bump(arena, 8)
        GF = b9.take(F32, [D])
        OT = [b9.take(F32, [D]) for _ in range(2)]
        JK9 = b9.take(BF16, [D])
        SS9 = [b9.take(F32, [1]) for _ in range(2)]
        assert b9.o <= 40 * KIB
        P.dma("sp", GF.ap, DR["gbc"][:, 4096:6144], writes=GF.k())
        outs = []
        def s1_fin(tt):
            ss = SS9[tt % 2]
            P.op("act", lambda e, tt=tt, ss=ss: e.activation(out=JK9.ap, in_=X1.ap[:, tt, :], func=AF.Square, accum_out=ss.ap),
                 reads=X1.r(tt * D, (tt + 1) * D), writes=JK9.k() + ss.k())
            rstd_from_ss(ss, ss.ap, ss.k(), 1.0 / D)

        s1_fin(0)
        for tt in range(8):
            ss = SS9[tt % 2]
            ot = OT[tt % 2]
            if tt + 1 < 8:
                s1_fin(tt + 1)
            P.op("dve", lambda e, tt=tt, ss=ss, ot=ot: e.scalar_tensor_tensor(
                out=ot.ap, in0=X1.ap[:, tt, :], scalar=ss.ap, in1=GF.ap, op0=ALU.mult, op1=ALU.mult),
                reads=X1.r(tt * D, (tt + 1) * D) + ss.k() + GF.k(), writes=ot.k())
            P.dma("sp", y[tt * 128:(tt + 1) * 128, :], ot.ap, reads=ot.k(), writes=[("y", tt)])
            outs.append(("y", tt))
        P.fence("sp", outs + (list(dbg.keys()) if debug else []))
        P.finish(st)
        build_program.stats = P.stats
    return nc


def _pk(w, kchunks):
    kp, c = w.shape
    return np.ascontiguousarray(w.reshape(kchunks, 128, c).transpose(1, 0, 2))


def _swap_halves(w, nheads, width):
    r = w.reshape(w.shape[0], nheads, 2, width // 2)
    return np.ascontiguousarray(r[:, :, ::-1, :]).reshape(w.shape[0], nheads * width)


def _na_bias_tables(rpb, core):
    H = 8
    ROWS, W, KR, KC = 128, 64, 8, 16
    out = np.full((H, 5, 768, 128), NEG, dtype=np.float32)
    blocks_of_slot = {0: 0, 1: 1, 2: 3, 3: 6, 4: 7}
    sb_of_block = [0, 0, 2, 4, 6, 8, 10, 12]
    col = np.arange(W)
    cstart = np.clip(col - KC // 2, 0, W - KC)
    for slot, blk in blocks_of_slot.items():
        for dq in range(2):
            r = core * 16 + 2 * blk + dq
            rs = int(np.clip(r - KR // 2, 0, ROWS - KR))
            for i in range(KR):
                rk = rs + i
                lr = rk - core * 16 + 4 - sb_of_block[blk]
                assert 0 <= lr < 12
                dr = rk - r + (KR - 1)
                for c in range(W):
                    q = dq * 64 + c
                    ck = cstart[c] + np.arange(KC)
                    dc = ck - c + (KC - 1)
                    out[:, slot, lr * 64 + ck, q] = rpb[:, dr, dc]
    out = out.reshape(H, 5, 6, 128, 128).transpose(0, 3, 1, 2, 4)
    return np.ascontiguousarray(out).reshape(H, 128, 5 * 6 * 128)


_CACHE = {}


def kernel(x, attn_norm_g, w_in, q_norm_g, w_uq, kv_norm_g, w_ukv, na_rpb, mla_out_norm_g, na_out_norm_g,
           w_out, mlp_norm_g, w_ff1, w_ff2, final_norm_g, _debug=False, _stop=None, _cores=None):
    f = np.float32
    x = np.asarray(x, f)[0]
    w_in = np.asarray(w_in, f)[0]
    w_uq = np.asarray(w_uq, f)[0]
    w_ukv = np.asarray(w_ukv, f)[0]
    w_out = np.asarray(w_out, f)[0]
    w_ff1 = np.asarray(w_ff1, f)[0]
    w_ff2 = np.asarray(w_ff2, f)[0]
    rpb = np.asarray(na_rpb, f)[0]
    g_attn = np.asarray(attn_norm_g, f)[0]
    g_q = np.asarray(q_norm_g, f)[0]
    g_kv = np.asarray(kv_norm_g, f)[0]
    g_mla = np.asarray(mla_out_norm_g, f)[0]
    g_na = np.asarray(na_out_norm_g, f)[0]
    g_mlp = np.asarray(mlp_norm_g, f)[0]
    g_fin = np.asarray(final_norm_g, f)

    half = 32
    inv = (10000.0 ** (-np.arange(half, dtype=f) / half)).astype(f)
    ang = (np.arange(S, dtype=f)[:, None] * inv[None, :]).astype(f)
    cosT = np.cos(ang).astype(f).T
    sinT = np.sin(ang).astype(f).T
    cs = np.stack([np.concatenate([cosT, cosT], 0), np.concatenate([-sinT, sinT], 0)], axis=1)

    xT = _pk(np.ascontiguousarray(x.T), 16)
    w_rope = w_in[:, 768:832]
    wa = _pk(np.concatenate([w_in[:, 512:832], _swap_halves(w_rope, 1, 64)], axis=1), 16)
    wcq = _pk(w_in[:, 0:512], 16)
    wna = _pk(w_in[:, 832:3904], 16)
    wuq = _pk(w_uq, 4)
    uq_rope = w_uq.reshape(512, 8, 192)[:, :, 128:192].reshape(512, 512)
    wuqs = _pk(_swap_halves(uq_rope, 8, 64), 4)
    wukv = _pk(w_ukv, 2)
    wout = _pk(w_out, 16)
    wff1 = _pk(w_ff1, 16)
    wff2 = _pk(w_ff2, 64)
    gvec = np.concatenate([g_attn.reshape(16, 128).T, g_q.reshape(4, 128).T, g_kv.reshape(2, 128).T], axis=1)
    gvec = np.ascontiguousarray(gvec, dtype=f)
    gbc = np.ascontiguousarray(np.broadcast_to(np.concatenate([g_mla, g_na, g_mlp, g_fin])[None, :], (128, 6144)), dtype=f)

    ident = np.eye(128, dtype=f)
    key = (bool(_debug), _stop)
    if key not in _CACHE:
        _CACHE[key] = build_program(debug=_debug, stop=_stop)
    nc = _CACHE[key]

    in_maps = []
    cores = list(range(NCORES)) if _cores is None else list(_cores)
    for c in cores:
        sh = c * TOK
        in_maps.append(dict(
            xT=np.ascontiguousarray(np.roll(xT, -sh, axis=2)),
            xown=np.ascontiguousarray(x[sh:sh + TOK]),
            cs=np.ascontiguousarray(np.roll(cs, -sh, axis=2)),
            wa=wa, wcq=wcq, wna=wna, wuq=wuq, wuqs=wuqs, wukv=wukv, wout=wout, wff1=wff1, wff2=wff2,
            gvec=gvec, gbc=gbc, nab=_na_bias_tables(rpb, c), ident=ident,
        ))
    res = run_bass_kernel_spmd(nc, in_maps, core_ids=list(range(len(cores))))
    if _debug:
        kernel.last = res
        if _cores is not None:
            return None
    out = np.concatenate([np.asarray(r["y"], dtype=f) for r in res.results], axis=0)
    return out.reshape(1, S, D)
```
